# Optimizing a Trainium2 kernel written in Bass

```python
import math
import jax, jax.numpy as jnp
from jax import lax
import numpy as np


D_MODEL = 2048
BATCH = 16
SEQ = 2048
DEPTH = 4

GRID_W = 64
CTX_LEN = 256
N_MIXERS = 3
Q_BLOCK = 128
ROPE_THETA = 10000.0
NORM_EPS = 1e-6

D_FF = 4 * D_MODEL

MLA_NOPE = 128
MLA_ROPE = 64
MLA_V = 128
MLA_HEADS = D_MODEL // 128
MLA_Q_RANK = 768
MLA_KV_RANK = 512

HY_SHORT = 3
HY_EMB_DIM = 33
HY_FILT_ORDER = 64
HY_TARGET = 1e-2
HY_FAST_PCT = 0.3
HY_SLOW_PCT = 1.5

DF_HEAD_DIM = 128
DF_HEADS = D_MODEL // (2 * DF_HEAD_DIM)
DF_SUBLN_EPS = 1e-5

N_LAYERS_A = (DEPTH + N_MIXERS - 1) // N_MIXERS
N_LAYERS_B = (DEPTH + N_MIXERS - 2) // N_MIXERS
N_LAYERS_C = (DEPTH + N_MIXERS - 3) // N_MIXERS

kernel_name = 'hybrid_mla_hyena_diffattn_dit'


def rmsnorm(x, g, eps=NORM_EPS):
    xf = x.astype(jnp.float32)
    y = xf * lax.rsqrt(jnp.mean(xf * xf, axis=-1, keepdims=True) + eps)
    return (y * g.astype(jnp.float32)).astype(x.dtype)


def modulate(x, shift, scale):
    return x * (1 + scale) + shift


def axial_rope_tables(L, rot_dim):
    rows = L // GRID_W
    row = jnp.repeat(jnp.arange(rows, dtype=jnp.float32), GRID_W)
    col = jnp.tile(jnp.arange(GRID_W, dtype=jnp.float32), rows)
    pos = jnp.stack([row, col], axis=-1)
    n_freq = rot_dim // 4
    inv_freq = ROPE_THETA ** (-jnp.arange(n_freq, dtype=jnp.float32) / n_freq)
    ang = pos[:, :, None, None] * inv_freq
    ang = jnp.broadcast_to(ang, (L, 2, 2, n_freq)).reshape(L, rot_dim)
    return jnp.cos(ang), jnp.sin(ang)


def apply_axial_rope(x, cos, sin):
    R = x.shape[-1]
    bshape = (cos.shape[0],) + (1,) * (x.ndim - 3) + (R,)
    cos = cos.reshape(bshape).astype(x.dtype)
    sin = sin.reshape(bshape).astype(x.dtype)
    xs = x.reshape(*x.shape[:-1], 2, 2, R // 4)
    rot = jnp.stack([-xs[..., 1, :], xs[..., 0, :]], axis=-2).reshape(x.shape)
    return x * cos + rot * sin


def attend(q, k, v, scale):
    s = jnp.einsum('bqhd,bshd->bhqs', q, k).astype(jnp.float32) * scale
    p = jax.nn.softmax(s, axis=-1).astype(v.dtype)
    return jnp.einsum('bhqs,bshd->bqhd', p, v)


def diff_attend(q, k, v, lam, scale):
    s = jnp.einsum('bqhcd,bshcd->bhcqs', q, k).astype(jnp.float32) * scale
    p = jax.nn.softmax(s, axis=-1)
    a = p[:, :, 0] - lam * p[:, :, 1]
    return jnp.einsum('bhqs,bshd->bqhd', a.astype(v.dtype), v)


def sweep_query_blocks(block_fn, *qs):
    B, L = qs[0].shape[:2]
    nb = L // Q_BLOCK
    blocks = tuple(jnp.moveaxis(q.reshape(B, nb, Q_BLOCK, *q.shape[2:]), 1, 0) for q in qs)
    out = lax.map(lambda qb: block_fn(*qb), blocks)
    return jnp.moveaxis(out, 0, 1).reshape(B, L, *out.shape[3:])


def mla_mixer(uc, ul, w_dq, q_norm_g, w_uq, w_dkv, kv_norm_g, w_ukv, w_o, cos, sin, need_ctx):
    B, C, _ = uc.shape
    L = ul.shape[1]
    S = C + L
    u = jnp.concatenate([uc, ul], axis=1)
    ckv_full = u @ w_dkv
    ckv = rmsnorm(ckv_full[..., :MLA_KV_RANK], kv_norm_g)
    kv = (ckv @ w_ukv).reshape(B, S, MLA_HEADS, MLA_NOPE + MLA_V)
    k_nope, v = kv[..., :MLA_NOPE], kv[..., MLA_NOPE:]
    k_rope = ckv_full[..., None, MLA_KV_RANK:]
    k_rope = jnp.concatenate([k_rope[:, :C], apply_axial_rope(k_rope[:, C:], cos, sin)], axis=1)
    k = jnp.concatenate([k_nope, jnp.broadcast_to(k_rope, (B, S, MLA_HEADS, MLA_ROPE))], axis=-1)
    scale = (MLA_NOPE + MLA_ROPE) ** -0.5

    def queries(us):
        q = rmsnorm(us @ w_dq, q_norm_g) @ w_uq
        return q.reshape(B, us.shape[1], MLA_HEADS, MLA_NOPE + MLA_ROPE)

    ql = queries(ul)
    ql = jnp.concatenate([ql[..., :MLA_NOPE], apply_axial_rope(ql[..., MLA_NOPE:], cos, sin)], axis=-1)
    ol = sweep_query_blocks(lambda qb: attend(qb, k, v, scale), ql)
    yl = ol.reshape(B, L, MLA_HEADS * MLA_V) @ w_o
    if not need_ctx:
        return None, yl
    oc = attend(queries(uc), k[:, :C], v[:, :C], scale)
    return oc.reshape(B, C, MLA_HEADS * MLA_V) @ w_o, yl


def short_conv(u, w, b):
    L = u.shape[1]
    pad = HY_SHORT // 2
    up = jnp.pad(u, ((0, 0), (pad, pad), (0, 0)))
    return sum(up[:, j:j + L] * w[j] for j in range(HY_SHORT)) + b


def hyena_filters(L, w1, b1, w2, b2, w3, b3, freq, w_out):
    t = jnp.linspace(0.0, 1.0, L, dtype=jnp.float32)[:, None]
    bands = (HY_EMB_DIM - 1) // 2
    w = 2.0 * math.pi * jnp.arange(L, dtype=jnp.float32)[:, None] / L
    f = jnp.linspace(1e-4, bands - 1, bands, dtype=jnp.float32)
    feats = jnp.concatenate([t, jnp.cos(f * w), -jnp.sin(f * w)], axis=-1)
    h = jnp.sin(freq[0] * (feats @ w1 + b1))
    h = jnp.sin(freq[1] * (h @ w2 + b2))
    h = jnp.sin(freq[2] * (h @ w3 + b3))
    h = (h @ w_out).astype(jnp.float32)
    deltas = jnp.abs(jnp.linspace(math.log(HY_TARGET) / HY_SLOW_PCT, math.log(HY_TARGET) / HY_FAST_PCT,
                                  D_MODEL, dtype=jnp.float32))
    decay = jnp.exp(-t * deltas)
    return h[:, :D_MODEL] * decay, h[:, D_MODEL:] * decay


def bidir_long_conv(u, h_fwd, h_bwd):
    L = u.shape[1]
    n = 2 * L
    k = jnp.concatenate([h_fwd, jnp.zeros_like(h_fwd[:1]), h_bwd[:0:-1]], axis=0)
    kf = jnp.fft.rfft(k, n=n, axis=0)
    uf = jnp.fft.rfft(u.astype(jnp.float32), n=n, axis=1)
    y = jnp.fft.irfft(uf * kf, n=n, axis=1)[:, :L]
    return y.astype(u.dtype)


def hyena_mixer(uc, ul, w_in, b_in, conv_w, conv_b, f_w1, f_b1, f_w2, f_b2, f_w3, f_b3, f_freq, f_wout,
                f_bias, w_out, b_out, need_ctx):
    def operator(u):
        L = u.shape[1]
        z = short_conv(u @ w_in + b_in, conv_w, conv_b)
        x0, x1, v = jnp.split(z, 3, axis=-1)
        h_f, h_b = hyena_filters(L, f_w1, f_b1, f_w2, f_b2, f_w3, f_b3, f_freq, f_wout)
        g = v * x1
        y = (bidir_long_conv(g, h_f, h_b) + g * f_bias) * x0
        return y @ w_out + b_out

    yl = operator(ul)
    if not need_ctx:
        return None, yl
    return operator(uc), yl


def diff_mixer(uc, ul, w_qkv, lambdas, subln_g, w_o, lambda_init, cos, sin, need_ctx):
    B, C, _ = uc.shape
    L = ul.shape[1]
    S = C + L
    qkv = jnp.concatenate([uc, ul], axis=1) @ w_qkv
    q, k, v = jnp.split(qkv, 3, axis=-1)
    q = q.reshape(B, S, DF_HEADS, 2, DF_HEAD_DIM)
    k = k.reshape(B, S, DF_HEADS, 2, DF_HEAD_DIM)
    v = v.reshape(B, S, DF_HEADS, 2 * DF_HEAD_DIM)
    q = jnp.concatenate([q[:, :C], apply_axial_rope(q[:, C:], cos, sin)], axis=1)
    k = jnp.concatenate([k[:, :C], apply_axial_rope(k[:, C:], cos, sin)], axis=1)
    lf = lambdas.astype(jnp.float32)
    lam = jnp.exp(jnp.sum(lf[0] * lf[1])) - jnp.exp(jnp.sum(lf[2] * lf[3])) + lambda_init
    scale = DF_HEAD_DIM ** -0.5

    def head_out(o):
        o = rmsnorm(o, subln_g, eps=DF_SUBLN_EPS) * (1.0 - lambda_init)
        return o.reshape(o.shape[0], o.shape[1], DF_HEADS * 2 * DF_HEAD_DIM) @ w_o

    ol = sweep_query_blocks(lambda qb: diff_attend(qb, k, v, lam, scale), q[:, C:])
    yl = head_out(ol)
    if not need_ctx:
        return None, yl
    oc = diff_attend(q[:, :C], k[:, :C], v[:, :C], lam, scale)
    return head_out(oc), yl


def sq_relu_mlp(x, w1, w2):
    return jnp.square(jax.nn.relu(x @ w1)) @ w2


def setup_inputs(seed: int = 0) -> dict:
    key = jax.random.key(seed)
    keys = jax.random.split(key, 40)
    cnt = [0]

    def nk():
        k = keys[cnt[0]]
        cnt[0] += 1
        return k

    def normal(shape):
        return jax.random.normal(nk(), shape, jnp.float32)

    def dense(shape, fan_in, gain=1.0):
        return normal(shape) * (gain * fan_in ** -0.5)

    def gains(shape):
        return 1.0 + 0.05 * normal(shape)

    def small(shape, s=0.01):
        return s * normal(shape)

    D = D_MODEL
    return {
        'x': normal((BATCH, SEQ, D)),
        'c': normal((BATCH, D)),
        'ctx': normal((BATCH, CTX_LEN, D)),
        'c_ctx': normal((D,)),
        'ada_w': dense((DEPTH, D, 6 * D), D, 0.5),
        'ada_b': small((DEPTH, 6 * D)),
        'norm_g': gains((DEPTH, 2, D)),
        'mlp_w1': dense((DEPTH, D, D_FF), D),
        'mlp_w2': dense((DEPTH, D_FF, D), D_FF),
        'final_norm_g': gains((D,)),
        'mla_w_dq': dense((N_LAYERS_A, D, MLA_Q_RANK), D),
        'mla_q_norm_g': gains((N_LAYERS_A, MLA_Q_RANK)),
        'mla_w_uq': dense((N_LAYERS_A, MLA_Q_RANK, MLA_HEADS * (MLA_NOPE + MLA_ROPE)), MLA_Q_RANK),
        'mla_w_dkv': dense((N_LAYERS_A, D, MLA_KV_RANK + MLA_ROPE), D),
        'mla_kv_norm_g': gains((N_LAYERS_A, MLA_KV_RANK)),
        'mla_w_ukv': dense((N_LAYERS_A, MLA_KV_RANK, MLA_HEADS * (MLA_NOPE + MLA_V)), MLA_KV_RANK),
        'mla_w_o': dense((N_LAYERS_A, MLA_HEADS * MLA_V, D), MLA_HEADS * MLA_V),
        'hy_w_in': dense((N_LAYERS_B, D, 3 * D), D),
        'hy_b_in': small((N_LAYERS_B, 3 * D)),
        'hy_conv_w': dense((N_LAYERS_B, HY_SHORT, 3 * D), HY_SHORT),
        'hy_conv_b': small((N_LAYERS_B, 3 * D)),
        'hy_filt_w1': dense((N_LAYERS_B, HY_EMB_DIM, HY_FILT_ORDER), HY_EMB_DIM),
        'hy_filt_b1': small((N_LAYERS_B, HY_FILT_ORDER), 0.1),
        'hy_filt_w2': dense((N_LAYERS_B, HY_FILT_ORDER, HY_FILT_ORDER), HY_FILT_ORDER),
        'hy_filt_b2': small((N_LAYERS_B, HY_FILT_ORDER), 0.1),
        'hy_filt_w3': dense((N_LAYERS_B, HY_FILT_ORDER, HY_FILT_ORDER), HY_FILT_ORDER),
        'hy_filt_b3': small((N_LAYERS_B, HY_FILT_ORDER), 0.1),
        'hy_filt_freq': gains((N_LAYERS_B, 3, HY_FILT_ORDER)),
        'hy_filt_wout': dense((N_LAYERS_B, HY_FILT_ORDER, 2 * D), HY_FILT_ORDER, 0.1),
        'hy_filt_bias': small((N_LAYERS_B, D), 0.5),
        'hy_w_out': dense((N_LAYERS_B, D, D), D),
        'hy_b_out': small((N_LAYERS_B, D)),
        'df_w_qkv': dense((N_LAYERS_C, D, 3 * D), D),
        'df_lambda': small((N_LAYERS_C, 4, DF_HEAD_DIM), 0.1),
        'df_subln_g': gains((N_LAYERS_C, 2 * DF_HEAD_DIM)),
        'df_w_o': dense((N_LAYERS_C, D, D), D),
    }


def reference(x, c, ctx, c_ctx, ada_w, ada_b, norm_g, mlp_w1, mlp_w2, final_norm_g,
              mla_w_dq, mla_q_norm_g, mla_w_uq, mla_w_dkv, mla_kv_norm_g, mla_w_ukv, mla_w_o,
              hy_w_in, hy_b_in, hy_conv_w, hy_conv_b, hy_filt_w1, hy_filt_b1, hy_filt_w2, hy_filt_b2,
              hy_filt_w3, hy_filt_b3, hy_filt_freq, hy_filt_wout, hy_filt_bias, hy_w_out, hy_b_out,
              df_w_qkv, df_lambda, df_subln_g, df_w_o):
    L = x.shape[1]
    cos_a, sin_a = axial_rope_tables(L, MLA_ROPE)
    cos_d, sin_d = axial_rope_tables(L, DF_HEAD_DIM)
    s_lat = jax.nn.silu(c)
    s_ctx = jax.nn.silu(c_ctx)
    hl, hc = x, ctx
    for i in range(DEPTH):
        need_ctx = i < DEPTH - 1
        sh1, sc1, g1, sh2, sc2, g2 = [m[:, None, :] for m in jnp.split(s_lat @ ada_w[i] + ada_b[i], 6, axis=-1)]
        csh1, csc1, cg1, csh2, csc2, cg2 = jnp.split(s_ctx @ ada_w[i] + ada_b[i], 6, axis=-1)
        ul = modulate(rmsnorm(hl, norm_g[i, 0]), sh1, sc1)
        uc = modulate(rmsnorm(hc, norm_g[i, 0]), csh1, csc1)
        kind, j = i % N_MIXERS, i // N_MIXERS
        if kind == 0:
            yc, yl = mla_mixer(uc, ul, mla_w_dq[j], mla_q_norm_g[j], mla_w_uq[j], mla_w_dkv[j],
                               mla_kv_norm_g[j], mla_w_ukv[j], mla_w_o[j], cos_a, sin_a, need_ctx)
        elif kind == 1:
            yc, yl = hyena_mixer(uc, ul, hy_w_in[j], hy_b_in[j], hy_conv_w[j], hy_conv_b[j],
                                 hy_filt_w1[j], hy_filt_b1[j], hy_filt_w2[j], hy_filt_b2[j],
                                 hy_filt_w3[j], hy_filt_b3[j], hy_filt_freq[j], hy_filt_wout[j],
                                 hy_filt_bias[j], hy_w_out[j], hy_b_out[j], need_ctx)
        else:
            lambda_init = 0.8 - 0.6 * math.exp(-0.3 * i)
            yc, yl = diff_mixer(uc, ul, df_w_qkv[j], df_lambda[j], df_subln_g[j], df_w_o[j],
                                lambda_init, cos_d, sin_d, need_ctx)
        hl = hl + g1 * yl
        hl = hl + g2 * sq_relu_mlp(modulate(rmsnorm(hl, norm_g[i, 1]), sh2, sc2), mlp_w1[i], mlp_w2[i])
        if need_ctx:
            hc = hc + cg1 * yc
            hc = hc + cg2 * sq_relu_mlp(modulate(rmsnorm(hc, norm_g[i, 1]), csh2, csc2), mlp_w1[i], mlp_w2[i])
    return rmsnorm(hl, final_norm_g)
```

```python
import math
from contextlib import ExitStack

import numpy as np
import ml_dtypes

import concourse.bass as bass
import concourse.mybir as mybir
from concourse.bass_utils import run_bass_kernel_spmd

F32 = mybir.dt.float32
BF16 = mybir.dt.bfloat16
AF = mybir.ActivationFunctionType
ALU = mybir.AluOpType

D = 2048
L = 2048
C = 256
NBATCH = 2
T = NBATCH * (L + C)
NBLK = T // 512
DFF = 8192
NCORES = 8
DEPTH = 4
NRING = 12
PI = math.pi


def blk_mod(blk):
    return 0 if blk < 4 else (1 if blk < 8 else 2)


class Ctx:
    ENG = ('pe', 'act', 'dve', 'pool', 'sp')

    def __init__(self, nc):
        self.nc = nc
        self.e = {'pe': nc.tensor, 'act': nc.scalar, 'dve': nc.vector, 'pool': nc.gpsimd, 'sp': nc.sync}
        self.sems = {}
        self.cnt = {}
        for k in self.ENG:
            self.sems[k] = nc.alloc_semaphore("c_" + k)
            self.cnt[k] = 0
        self.dq = {}
        for q, eng in (('sp', 'sp'), ('pool', 'pool'), ('cv', 'pool')):
            names = []
            for i in range(NRING):
                nm = "d_%s%d" % (q, i)
                self.sems[nm] = nc.alloc_semaphore(nm)
                self.cnt[nm] = 0
                names.append(nm)
            self.dq[q] = [names, 0, eng]
        self.known = {k: {} for k in self.ENG}
        self.res_w = {}
        self.res_r = {}
        self.n_inst = 0
        self.n_wait = 0
        self.pe_pending = False

    def _wait(self, eng, ev):
        sk, v = ev
        if sk == 'pe' and eng == 'pe':
            return
        if self.known[eng].get(sk, 0) >= v:
            return
        assert self.cnt[sk] >= v, ("dependency on pending signal", eng, ev, self.cnt[sk])
        self.e[eng].wait_ge(self.sems[sk], v)
        self.known[eng][sk] = v
        self.n_wait += 1

    def _deps(self, eng, reads, writes):
        for r in reads:
            ev = self.res_w.get(r)
            if ev is not None:
                self._wait(eng, ev)
        for w in writes:
            ev = self.res_w.get(w)
            if ev is not None:
                self._wait(eng, ev)
            rd = self.res_r.get(w)
            if rd:
                for sk, v in rd.items():
                    self._wait(eng, (sk, v))

    def _record(self, ev, reads, writes):
        sk, v = ev
        for r in reads:
            d = self.res_r.setdefault(r, {})
            if d.get(sk, 0) < v:
                d[sk] = v
        for w in writes:
            self.res_w[w] = ev
            self.res_r[w] = {}

    def op(self, eng, fn, reads=(), writes=(), signal=True):
        self._deps(eng, reads, writes)
        inst = fn(self.e[eng])
        self.n_inst += 1
        if signal:
            self.cnt[eng] += 1
            inst.then_inc(self.sems[eng], 1)
            ev = (eng, self.cnt[eng])
            if eng == 'pe':
                self.pe_pending = False
        else:
            assert eng == 'pe'
            ev = (eng, self.cnt[eng] + 1)
            self.pe_pending = True
        self._record(ev, reads, writes)
        return inst

    def dma(self, q, out, in_, reads=(), writes=(), **kw):
        names, idx, eng = self.dq[q]
        nm = names[idx % NRING]
        self.dq[q][1] = idx + 1
        if self.cnt[nm] > 0:
            self._wait(eng, (nm, self.cnt[nm]))
        self._deps(eng, reads, writes)
        inst = self.e[eng].dma_start(out=out, in_=in_, **kw)
        self.cnt[nm] += 16
        inst.then_inc(self.sems[nm], 16)
        self.n_inst += 1
        self._record((nm, self.cnt[nm]), reads, writes)
        return inst

    def barrier(self, keep=('wbf', 'bgin', 'bgout')):
        assert not self.pe_pending
        for eng in self.ENG:
            for sk in self.sems:
                if sk.startswith('d_cv'):
                    continue
                if self.cnt[sk] > 0:
                    self._wait(eng, (sk, self.cnt[sk]))
        self.res_w = {k: v for k, v in self.res_w.items() if k[0] in keep}
        self.res_r = {k: v for k, v in self.res_r.items() if k[0] in keep}

    def final_wait(self):
        assert not self.pe_pending
        for sk in self.sems:
            if self.cnt[sk] > 0:
                self._wait('sp', (sk, self.cnt[sk]))


def _pm(v):
    v = np.asarray(v)
    return np.ascontiguousarray(v.reshape(-1, 128).T)


def _rope_tables(rot):
    rows = L // 64
    row = np.repeat(np.arange(rows, dtype=np.float32), 64)
    col = np.tile(np.arange(64, dtype=np.float32), rows)
    pos = np.stack([row, col], -1)
    nf = rot // 4
    inv = (np.float32(10000.0) ** (-np.arange(nf, dtype=np.float32) / np.float32(nf))).astype(np.float32)
    ang = pos[:, :, None, None] * inv
    ang = np.broadcast_to(ang, (L, 2, 2, nf)).reshape(L, rot)
    return np.ascontiguousarray(np.cos(ang).astype(np.float32).T), np.ascontiguousarray(np.sin(ang).astype(np.float32).T)


def _rot_lhsT(rot):
    nf = rot // 4
    Rm = np.zeros((rot, rot), np.float32)
    for a in range(2):
        for f in range(nf):
            i0 = a * 2 * nf + f
            i1 = a * 2 * nf + nf + f
            Rm[i0, i1] = -1.0
            Rm[i1, i0] = 1.0
    return np.ascontiguousarray(Rm.T)


def _dft_consts(Lx):
    n = 2 * Lx
    t = np.arange(Lx, dtype=np.float64)
    ang = (2.0 * np.pi / n) * np.outer(t, t)
    CM = np.cos(ang)
    SF = np.sin(ang)
    SF[:, 0] = (-1.0) ** t
    nft = Lx // 128
    WF = np.zeros((Lx, 2 * Lx), np.float64)
    for ft in range(nft):
        WF[:, ft * 256: ft * 256 + 128] = CM[:, ft * 128:(ft + 1) * 128]
        WF[:, ft * 256 + 128: ft * 256 + 256] = SF[:, ft * 128:(ft + 1) * 128]
    CSI = np.concatenate([CM, SF.T], 0)
    bf = ml_dtypes.bfloat16
    return WF.astype(np.float32).astype(bf), CSI.astype(np.float32).astype(bf), np.ascontiguousarray(SF[:, 0:2]).astype(np.float32).astype(bf)


def _hy_consts(Lx):
    t = np.linspace(0.0, 1.0, Lx, dtype=np.float32)[:, None]
    bands = 16
    w = (np.float32(2.0 * math.pi) * np.arange(Lx, dtype=np.float32)[:, None] / np.float32(Lx)).astype(np.float32)
    f = np.linspace(1e-4, bands - 1, bands, dtype=np.float32)
    feats = np.concatenate([t, np.cos(f * w), -np.sin(f * w)], -1).astype(np.float32)
    deltas = np.abs(np.linspace(math.log(1e-2) / 1.5, math.log(1e-2) / 0.3, D, dtype=np.float32))
    decay = np.exp(-t * deltas).astype(np.float32)
    return np.ascontiguousarray(feats.T), decay


_CONST_CACHE = {}


def _constants():
    if _CONST_CACHE:
        return _CONST_CACHE
    cc = {}
    cc['k_ones'] = np.ones((128, 128), np.float32).astype(ml_dtypes.bfloat16)
    cc['k_onesf'] = np.ones((128, 128), np.float32)
    cc['k_ident'] = np.eye(128, dtype=np.float32)
    ca, sa = _rope_tables(64)
    cd, sd = _rope_tables(128)
    cc['k_cosa'] = ca
    cc['k_sina'] = sa
    cc['k_cosd'] = cd
    cc['k_sind'] = sd
    cc['k_rota'] = _rot_lhsT(64)
    cc['k_rotd'] = _rot_lhsT(128)
    for tag, Lx in (('l', L), ('c', C)):
        WF, CSI, SG = _dft_consts(Lx)
        cc['k_wf' + tag] = WF
        cc['k_csi' + tag] = CSI
        cc['k_sg' + tag] = SG
        ft, dec = _hy_consts(Lx)
        cc['k_feat' + tag] = ft
        cc['k_dec' + tag] = dec
    _CONST_CACHE.update(cc)
    return cc


class Builder:
    def __init__(self, layers=(0, 1, 2, 3), final_norm=True, dbg=()):
        self.layers = list(layers)
        self.final_norm = final_norm
        self.dbg = set(dbg)
        self.nc = bass.Bass("TRN2", target_bir_lowering=False)
        self.cx = Ctx(self.nc)
        self.inputs = {}
        self.psn = 0
        self.consts = _constants()

    def din(self, name, shape, dt=F32):
        t = self.nc.dram_tensor(name, list(shape), dt, kind="ExternalInput").ap()
        self.inputs[name] = (tuple(shape), dt)
        return t

    def dscr(self, name, shape, dt, out=False):
        kind = "ExternalOutput" if (out or name in self.dbg) else "Internal"
        return self.nc.dram_tensor(name, list(shape), dt, kind=kind).ap()

    def sb(self, st, name, shape, dt):
        self._sbn = getattr(self, '_sbn', 0) + 1
        return st.enter_context(self.nc.sbuf_tensor("%s_%d" % (name, self._sbn), list(shape), dt))

    def next_ps(self, lo=0, hi=6):
        b = lo + (self.psn % (hi - lo))
        self.psn += 1
        return b

    def build(self):
        nc, cx = self.nc, self.cx
        ly = self.layers
        self.xT = self.din("xT", [D, T])
        self.cT = self.din("cT", [128, 16, 3])
        W = {}
        W['ada_w'] = self.din("ada_w", [DEPTH, D, 6 * D])
        W['mlp_w1'] = self.din("mlp_w1", [DEPTH, D, DFF])
        W['mlp_w2'] = self.din("mlp_w2", [DEPTH, DFF, D])
        W['mla_w_dq'] = self.din("mla_w_dq", [2, D, 768])
        W['mla_w_uq'] = self.din("mla_w_uq", [2, 768, 3072])
        W['mla_w_dkv'] = self.din("mla_w_dkv", [2, D, 576])
        W['mla_w_ukv'] = self.din("mla_w_ukv", [2, 512, 4096])
        W['mla_w_o'] = self.din("mla_w_o", [2, D, D])
        W['hy_w_in'] = self.din("hy_w_in", [1, D, 3 * D])
        W['hy_w_out'] = self.din("hy_w_out", [1, D, D])
        W['df_w_qkv'] = self.din("df_w_qkv", [1, D, 3 * D])
        W['df_w_o'] = self.din("df_w_o", [1, D, D])
        self.W = W
        self.v_adab = self.din("v_adab", [128, DEPTH * 96])
        self.v_normg = self.din("v_normg", [128, DEPTH * 2 * 16])
        self.v_fing = self.din("v_fing", [128, 16])
        self.v_qng = self.din("v_qng", [128, 2 * 6])
        self.v_kvng = self.din("v_kvng", [128, 2 * 4])
        self.v_hybin = self.din("v_hybin", [128, 48])
        self.v_hycw = self.din("v_hycw", [128, 3 * 48])
        self.v_hycb = self.din("v_hycb", [128, 48])
        self.v_hyfb = self.din("v_hyfb", [128, 16])
        self.v_hybout = self.din("v_hybout", [128, 16])
        self.v_subg = self.din("v_subg", [128, 2])
        self.v_lam = self.din("v_lam", [1, 512])
        self.f_w1 = self.din("f_w1", [33, 64])
        self.f_w2 = self.din("f_w2", [64, 64])
        self.f_w3 = self.din("f_w3", [64, 64])
        self.f_vec = self.din("f_vec", [64, 6])
        self.f_wout = self.din("f_wout", [64, 2 * D])
        self.K = {}
        for k, v in self.consts.items():
            dt = BF16 if v.dtype == ml_dtypes.bfloat16 else F32
            self.K[k] = self.din(k, v.shape, dt)
        self.H = self.dscr("H", [D, T], F32)
        self.U = self.dscr("U", [D, T], BF16)
        self.OUT = self.nc.dram_tensor("OUT", [D, NBATCH * L], F32, kind="ExternalOutput").ap()
        self.wbf = {}
        self.wbf_names = set()

        with ExitStack() as st:
            self.ps = [nc.alloc_psum_tensor("ps%d" % i, [128, 512], F32) for i in range(8)]
            self.ones = self.sb(st, "ones", [128, 128], BF16)
            self.onesf = self.sb(st, "onesf", [128, 128], F32)
            self.ident = self.sb(st, "ident", [128, 128], F32)
            self.sT = self.sb(st, "sT", [128, 16, 4], BF16)
            self.sT32 = self.sb(st, "sT32", [128, 16, 4], F32)
            self.modsb = self.sb(st, "modsb", [128, 96, 4], F32)
            self.A = self.sb(st, "Amod", [128, 2, 16, 4], F32)
            self.adab = self.sb(st, "adab", [128, DEPTH * 96], F32)
            self.normg = self.sb(st, "normg", [128, DEPTH * 2 * 16], F32)
            self.fing = self.sb(st, "fing", [128, 16], F32)
            self.bg_in = [self.sb(st, "bg_in%d" % k, [128, self.BGW], F32) for k in range(2)]
            self.bg_out = [self.sb(st, "bg_out%d" % k, [128, self.BGW], BF16) for k in range(2)]
            self.bg_q = []
            self.bg_cur = None
            self.bg_n = 0
            self.bg_done = set()
            self._epsb = {}
            for eps in (1e-6, 1e-5):
                t = self.sb(st, "epsb%d" % len(self._epsb), [128, 1], F32)
                cx.op('pool', lambda e: e.memset(t[:, :], float(eps)), writes=['epsb'])
                self._epsb[eps] = t
            cx.dma('sp', self.ones[:, :], self.K['k_ones'][:, :], writes=['ones'])
            cx.dma('sp', self.onesf[:, :], self.K['k_onesf'][:, :], writes=['onesf'])
            cx.dma('sp', self.ident[:, :], self.K['k_ident'][:, :], writes=['ident'])
            cx.dma('sp', self.adab[:, :], self.v_adab[:, :], writes=['adab'])
            cx.dma('sp', self.normg[:, :], self.v_normg[:, :], writes=['normg'])
            cx.dma('sp', self.fing[:, :], self.v_fing[:, :], writes=['fing'])
            with ExitStack() as s2:
                ct = self.sb(s2, "ct", [128, 16, 3], F32)
                cx.dma('sp', ct[:, :, :], self.cT[:, :, :], writes=['ct'])
                cx.op('pool', lambda e: e.memset(self.sT32[:, :, :], 0.0), writes=['sT'])
                cx.op('act', lambda e: e.activation(out=self.sT32[:, :, 0:3], in_=ct[:, :, :], func=AF.Silu),
                      reads=['ct', 'sT'], writes=['sT'])
                cx.op('act', lambda e: e.activation(out=self.sT[:, :, :], in_=self.sT32[:, :, :], func=AF.Copy),
                      reads=['sT'], writes=['sT'])
                cx.barrier()
            for i in ly:
                self.convert_layer(i)
            hsrc = self.xT
            for i in ly:
                self.adaln(i)
                need_ctx = i < DEPTH - 1
                self.norm_phase(i, 0, hsrc, list(range(NBLK)))
                kind, j = i % 3, i // 3
                if kind == 0:
                    self.mla(i, j, hsrc, need_ctx)
                elif kind == 1:
                    self.hyena(i, j, hsrc, need_ctx)
                else:
                    self.diffattn(i, j, hsrc, need_ctx)
                blks = list(range(NBLK if need_ctx else 8))
                if not need_ctx and hsrc is not self.H:
                    pass
                hsrc = self.H
                self.mlp(i, blks)
            if self.final_norm:
                self.norm_phase(None, 2, hsrc, list(range(8)))
            cx.final_wait()
        return nc

    BGW = 1024

    def convert(self, name, idx):
        src = self.W[name][idx]
        Kd, Fd = src.shape
        dst = self.dscr("wb_%s%d" % (name, idx), [Kd, Fd], BF16)
        keys = []
        BGW = self.BGW
        if name == 'mla_w_ukv':
            sv = src.rearrange("k (h two d) -> k two h d", two=2, d=128)
            dv = dst.rearrange("k (two h d) -> k two h d", two=2, d=128)
            for r0 in range(0, Kd, 128):
                for two in range(2):
                    for hh in range(0, 16, BGW // 128):
                        key = ('wbf', name, idx, r0, two, hh)
                        keys.append(key)
                        self.bg_q.append((name, idx, sv[r0:r0 + 128, two, hh:hh + BGW // 128, :], dv[r0:r0 + 128, two, hh:hh + BGW // 128, :], key, True))
        else:
            n = Kd * Fd
            assert n % (128 * BGW) == 0, (name, n)
            sv = src.rearrange("k f -> (k f)").rearrange("(r c) -> r c", c=BGW)
            dv = dst.rearrange("k f -> (k f)").rearrange("(r c) -> r c", c=BGW)
            for r0 in range(0, n // BGW, 128):
                key = ('wbf', name, idx, r0)
                keys.append(key)
                self.bg_q.append((name, idx, sv[r0:r0 + 128, :], dv[r0:r0 + 128, :], key, False))
        self.wbf[(name, idx)] = (dst, keys)
        self.wbf_names.add((name, idx))

    def _bg_load(self, unit, slot):
        name, idx, sv, dv, key, three = unit
        t = self.bg_in[slot]
        dstv = t[:, :].rearrange("p (h d) -> p h d", d=128) if three else t[:, :]
        self.cx.dma('cv', dstv, sv, writes=[('bgin', slot)])

    def bg_step(self, n=1):
        cx = self.cx
        for _ in range(n):
            if self.bg_cur is None:
                if not self.bg_q:
                    return
                self.bg_cur = (self.bg_q.pop(0), self.bg_n % 2)
                self._bg_load(*self.bg_cur)
                self.bg_n += 1
            unit, slot = self.bg_cur
            nxt = None
            if self.bg_q:
                nxt = (self.bg_q.pop(0), self.bg_n % 2)
                self._bg_load(*nxt)
                self.bg_n += 1
            name, idx, sv, dv, key, three = unit
            cx.op('pool', lambda e: e.tensor_copy(out=self.bg_out[slot][:, :], in_=self.bg_in[slot][:, :]),
                  reads=[('bgin', slot)], writes=[('bgout', slot)])
            srcv = self.bg_out[slot][:, :].rearrange("p (h d) -> p h d", d=128) if three else self.bg_out[slot][:, :]
            cx.dma('cv', dv, srcv, reads=[('bgout', slot)], writes=[key])
            self.bg_done.add(key)
            self.bg_cur = nxt

    def need(self, name, idx):
        keys = self.wbf[(name, idx)][1]
        while not all(k in self.bg_done for k in keys):
            self.bg_step(1)
        return self.wbf[(name, idx)]

    def convert_layer(self, i):
        kind, j = i % 3, i // 3
        if i != self.layers[0]:
            self.convert('ada_w', i)
        if kind == 0:
            for nm in ('mla_w_dkv', 'mla_w_ukv', 'mla_w_dq', 'mla_w_uq', 'mla_w_o'):
                self.convert(nm, j)
        elif kind == 1:
            for nm in ('hy_w_in', 'hy_w_out'):
                self.convert(nm, j)
        else:
            for nm in ('df_w_qkv', 'df_w_o'):
                self.convert(nm, j)
        self.convert('mlp_w1', i)
        self.convert('mlp_w2', i)

    def gemm(self, st, Wd, wkeys, KC, ftiles, blocks, xview, epi, x_sb=None, ps_hi=6, token_major=False, wdt=None, xel=16384):
        cx = self.cx
        wdt = wdt or BF16
        wel = 8192 if wdt == BF16 else 4096
        Wv = Wd.rearrange("(kc p) f -> p kc f", p=128)
        maxw = min(1024, wel // KC)
        if token_major:
            maxw = min(512, maxw)
        pieces = []
        cur = None
        for ti, (f0, fsz) in enumerate(ftiles):
            if cur is not None and cur[0] + cur[1] == f0 and cur[1] + fsz <= maxw:
                cur[1] += fsz
                cur[2].append((ti, f0, fsz))
            else:
                cur = [f0, fsz, [(ti, f0, fsz)]]
                pieces.append(cur)
        NWS = len(pieces) if len(pieces) <= 4 else 3
        wp = [self.sb(st, "g_wp%d" % i, [128, wel], wdt) for i in range(NWS)]
        nw = [0]

        def load_piece(pi, slot):
            pf0, pw, tiles = pieces[pi]
            wv = wp[slot][:, 0:KC * pw].rearrange("p (kc w) -> p kc w", kc=KC)
            cx.dma('sp', wv, Wv[:, :, pf0:pf0 + pw], reads=wkeys, writes=[('wp', slot)])
            return wv

        def run_tiles(wv, ws, pf0, tiles, subs):
            self.bg_step()
            for (ti, f0, fsz) in tiles:
                o = f0 - pf0
                for (blk, n, xv, xkey) in subs:
                    if not token_major:
                        b = self.next_ps(0, ps_hi)
                        pst = self.ps[b][0:fsz, 0:n]
                        for kc in range(KC):
                            cx.op('pe', lambda e: e.matmul(pst, lhsT=wv[:, kc, o:o + fsz], rhs=xv[:, kc, :],
                                                           start=(kc == 0), stop=(kc == KC - 1)),
                                  reads=[('wp', ws), xkey], writes=[('ps', b)], signal=(kc == KC - 1))
                        epi(blk, ti, f0, fsz, pst, ('ps', b), n)
                    else:
                        for tt in range(n // 128):
                            b = self.next_ps(0, ps_hi)
                            pst = self.ps[b][:, 0:fsz]
                            for kc in range(KC):
                                cx.op('pe', lambda e: e.matmul(pst, lhsT=xv[:, kc, tt * 128:(tt + 1) * 128],
                                                               rhs=wv[:, kc, o:o + fsz],
                                                               start=(kc == 0), stop=(kc == KC - 1)),
                                      reads=[('wp', ws), xkey], writes=[('ps', b)], signal=(kc == KC - 1))
                            epi(blk, ti, f0, fsz, pst, ('ps', b), n, tt)

        LOOK = NWS - 1 if NWS > 1 else 0

        def stream(n_outer, subs_fn, resident=False, pre=None):
            items = [(so, pi) for so in range(n_outer) for pi in range(len(pieces))]
            views = {}
            issued = [0]

            def ensure(k):
                while issued[0] <= k and issued[0] < len(items):
                    so, pi = items[issued[0]]
                    if resident:
                        if so == 0:
                            views[(so, pi)] = (load_piece(pi, pi), pi)
                        else:
                            views[(so, pi)] = views[(0, pi)]
                    else:
                        slot = issued[0] % NWS
                        views[(so, pi)] = (load_piece(pi, slot), slot)
                    issued[0] += 1
            for t, (so, pi) in enumerate(items):
                ensure(t + LOOK)
                if pi == 0 and pre is not None:
                    pre(so)
                wv, ws = views.pop((so, pi)) if not resident else views[(so, pi)]
                pf0, pw, tiles = pieces[pi]
                run_tiles(wv, ws, pf0, tiles, subs_fn(so))

        if x_sb is not None:
            blk, n = blocks[0]
            stream(1, lambda so: [(blk, n, x_sb[0], x_sb[1])])
            return
        xtot = sum(KC * n for (_, n) in blocks)
        if xtot <= 32768 and wdt == BF16:
            xall = self.sb(st, "g_xall", [128, 32768], BF16)
            subs = []
            off = 0
            for (blk, n) in blocks:
                xv = xall[:, off:off + KC * n].rearrange("p (kc n) -> p kc n", kc=KC)
                cx.dma('sp', xv, xview(blk), writes=[('xall', blk)])
                subs.append((blk, n, xv, ('xall', blk)))
                off += KC * n
            stream(1, lambda so: subs)
            return
        resident = len(pieces) <= NWS
        nxs = 2 if xel <= 16384 else 1
        xin = [self.sb(st, "g_xin%d" % i, [128, xel], BF16) for i in range(nxs)]
        sbl = []
        cap = xel // KC
        for (blk, n) in blocks:
            if sbl and sbl[-1][1] + n <= min(cap, 1024) and sbl[-1][0][-1][0] + 1 == blk and blk_mod(sbl[-1][0][-1][0]) == blk_mod(blk):
                sbl[-1][0].append((blk, n))
                sbl[-1][1] += n
            else:
                sbl.append([[(blk, n)], n])
        xsubs = {}

        def load_x(si):
            s = si % nxs
            subs = []
            off = 0
            for (blk, n) in sbl[si][0]:
                xv = xin[s][:, off:off + KC * n].rearrange("p (kc n) -> p kc n", kc=KC)
                cx.dma('sp', xv, xview(blk), writes=[('xin', s, off)])
                subs.append((blk, n, xv, ('xin', s, off)))
                off += KC * n
            xsubs[si] = subs

        def pre(si):
            if si not in xsubs:
                load_x(si)
            if nxs == 2 and si + 1 < len(sbl) and (si + 1) not in xsubs:
                load_x(si + 1)
        load_x(0)
        stream(len(sbl), lambda si: xsubs[si], resident=resident, pre=pre)

    def fm_rstd(self, st_bufs, src, skey, nch, n, dim, eps, rows=128, bank=None):
        cx = self.cx
        sq = st_bufs['sq']
        rstd = st_bufs['rstd']
        tg = st_bufs.get('tag', 0)
        ksq = ('sq', tg)
        krs = ('rstd', tg)
        sqv = sq[0:rows, 0:nch * n].rearrange("p (c n) -> p c n", c=nch)
        cx.op('act', lambda e: e.activation(out=sqv, in_=src, func=AF.Square), reads=[skey], writes=[ksq])
        b = self.next_ps(6, 8) if bank is None else bank
        pst = self.ps[b][0:rows, 0:n]
        for kc in range(nch):
            cx.op('pe', lambda e: e.matmul(pst, lhsT=self.ones[0:rows, 0:rows], rhs=sqv[:, kc, :],
                                           start=(kc == 0), stop=(kc == nch - 1)),
                  reads=[ksq, 'ones'], writes=[('ps', b)], signal=(kc == nch - 1))
        cx.op('act', lambda e: e.activation(out=rstd[0:rows, 0:n], in_=pst, func=AF.Sqrt, scale=1.0 / dim, bias=self.epsb(eps)[0:rows, :]),
              reads=[('ps', b), 'epsb'], writes=[krs])
        cx.op('dve', lambda e: e.reciprocal(out=rstd[0:rows, 0:n], in_=rstd[0:rows, 0:n]), reads=[krs], writes=[krs])
        return rstd[0:rows, 0:n], krs

    def epsb(self, eps):
        return self._epsb[eps]

    def adaln(self, i):
        cx = self.cx
        fp32_path = ('ada_w', i) not in self.wbf_names
        if fp32_path:
            Wd, wk = self.W['ada_w'][i], []
            wdt, wel, pw, sx = F32, 4096, 256, self.sT32
        else:
            Wd, wk = self.need('ada_w', i)
            wdt, wel, pw, sx = BF16, 8192, 512, self.sT
        Wv = Wd.rearrange("(kc p) f -> p kc f", p=128)
        npc = 6 * D // pw
        with ExitStack() as st:
            wp = [self.sb(st, "al_wp%d" % k, [128, wel], wdt) for k in range(3)]
            modrow = self.sb(st, "al_row", [4, 6 * D], F32)
            views = {}
            issued = [0]

            def ensure(k):
                while issued[0] <= k and issued[0] < npc:
                    p_ = issued[0]
                    wv = wp[p_ % 3][:, :].rearrange("p (kc w) -> p kc w", kc=16)
                    cx.dma('sp', wv, Wv[:, :, p_ * pw:(p_ + 1) * pw], reads=wk, writes=[('alw', p_ % 3)])
                    views[p_] = wv
                    issued[0] += 1
            for pi in range(npc):
                ensure(pi + 2)
                self.bg_step(1)
                wv = views.pop(pi)
                bk = self.next_ps(0, 6)
                pst = self.ps[bk][0:4, 0:pw]
                for kc in range(16):
                    cx.op('pe', lambda e: e.matmul(pst, lhsT=sx[:, kc, 0:4], rhs=wv[:, kc, :], start=(kc == 0), stop=(kc == 15)),
                          reads=[('alw', pi % 3), 'sT'], writes=[('ps', bk)], signal=(kc == 15))
                cx.op('act', lambda e: e.activation(out=modrow[0:4, pi * pw:(pi + 1) * pw], in_=pst, func=AF.Copy),
                      reads=[('ps', bk)], writes=['modrow'])
            bk = self.next_ps(6, 8)
            for ft in range(96):
                cx.op('pe', lambda e: e.transpose(self.ps[bk][:, ft * 4:(ft + 1) * 4], modrow[0:4, ft * 128:(ft + 1) * 128], self.ident[0:4, 0:4]),
                      reads=['modrow', 'ident'], writes=[('ps', bk)], signal=(ft == 95))
            psv = self.ps[bk][:, 0:384].rearrange("p (f c) -> p f c", c=4)
            for col in range(3):
                cx.op('dve', lambda e: e.tensor_tensor(out=self.modsb[:, :, col], in0=psv[:, :, col], in1=self.adab[:, i * 96:(i + 1) * 96], op=ALU.add),
                      reads=[('ps', bk), 'adab'], writes=['modsb'])
            for w in range(2):
                sc0 = 16 + 48 * w
                for col in range(3):
                    cx.op('dve', lambda e: e.scalar_tensor_tensor(
                        out=self.A[:, w, :, col], in0=self.modsb[:, sc0:sc0 + 16, col], scalar=1.0,
                        in1=self.normg[:, (i * 2 + w) * 16:(i * 2 + w + 1) * 16], op0=ALU.add, op1=ALU.mult),
                        reads=['modsb', 'normg'], writes=['A'])
            cx.barrier()

    def m_shift(self, w, kc, col):
        return self.modsb[:, 48 * w + kc, col:col + 1]

    def m_gate(self, w, kc, col):
        return self.modsb[:, 32 + 48 * w + kc, col:col + 1]

    def norm_phase(self, i, w, hsrc, blks):
        cx = self.cx
        hv = hsrc.rearrange("(kc p) t -> p kc t", p=128)
        uv = self.U.rearrange("(kc p) t -> p kc t", p=128)
        ov = self.OUT.rearrange("(kc p) t -> p kc t", p=128)
        with ExitStack() as st:
            xt = [self.sb(st, "n_xt%d" % k, [128, 16, 512], F32) for k in range(2)]
            bufs2 = [{'sq': self.sb(st, "n_sq%d" % k, [128, 8192], BF16), 'rstd': self.sb(st, "n_rstd%d" % k, [128, 512], F32), 'tag': k}
                     for k in range(2)]
            if w < 2:
                ut = [self.sb(st, "n_ut%d" % k, [128, 16, 512], BF16) for k in range(2)]
            cx.dma('sp', xt[0][:, :, :], hv[:, :, blks[0] * 512:(blks[0] + 1) * 512], writes=[('xt', 0)])
            for bi, blk in enumerate(blks):
                s = bi % 2
                col = blk_mod(blk)
                if bi + 1 < len(blks):
                    nb = blks[bi + 1]
                    cx.dma('sp', xt[1 - s][:, :, :], hv[:, :, nb * 512:(nb + 1) * 512], writes=[('xt', 1 - s)])
                rstd, rk = self.fm_rstd(bufs2[s], xt[s][:, :, :], ('xt', s), 16, 512, D, 1e-6)
                for kc in range(16):
                    cx.op('dve', lambda e: e.tensor_tensor(out=xt[s][:, kc, :], in0=xt[s][:, kc, :], in1=rstd, op=ALU.mult),
                          reads=[rk, ('xt', s)], writes=[('xt', s)])
                if w < 2:
                    for kc in range(16):
                        cx.op('act', lambda e: e.activation(out=ut[s][:, kc, :], in_=xt[s][:, kc, :], func=AF.Identity,
                                                            scale=self.A[:, w, kc, col:col + 1], bias=self.m_shift(w, kc, col)),
                              reads=[('xt', s), 'A', 'modsb'], writes=[('ut', s)])
                    cx.dma('sp', uv[:, :, blk * 512:(blk + 1) * 512], ut[s][:, :, :], reads=[('ut', s)], writes=[('U', blk)])
                else:
                    for kc in range(16):
                        cx.op('act', lambda e: e.activation(out=xt[s][:, kc, :], in_=xt[s][:, kc, :], func=AF.Copy,
                                                            scale=self.fing[:, kc:kc + 1]),
                              reads=[('xt', s), 'fing'], writes=[('xt', s)])
                    cx.dma('sp', ov[:, :, blk * 512:(blk + 1) * 512], xt[s][:, :, :], reads=[('xt', s)], writes=[('OUT', blk)])
            cx.barrier()

    def make_res_epi(self, st, w, hsrc, bias_fn=None, nslots=3):
        cx = self.cx
        hr = [self.sb(st, "r_hr%d" % k, [128, 512], F32) for k in range(nslots)]
        cnt = [0]

        def epi(blk, ti, f0, fsz, pst, pskey, n):
            s = cnt[0] % nslots
            cnt[0] += 1
            col = blk_mod(blk)
            c0 = blk * 512
            cx.dma('sp', hr[s][:, 0:n], hsrc[f0:f0 + 128, c0:c0 + n], writes=[('hr', s)])
            gate = self.m_gate(w, ti, col)
            if bias_fn is None:
                cx.op('dve', lambda e: e.scalar_tensor_tensor(out=hr[s][:, 0:n], in0=pst, scalar=gate, in1=hr[s][:, 0:n],
                                                              op0=ALU.mult, op1=ALU.add),
                      reads=[pskey, ('hr', s), 'modsb'], writes=[('hr', s)])
            else:
                tmp = self._res_tmp
                cx.op('act', lambda e: e.activation(out=tmp[:, 0:n], in_=pst, func=AF.Identity, bias=bias_fn(ti)),
                      reads=[pskey, 'hyvec'], writes=['res_tmp'])
                cx.op('dve', lambda e: e.scalar_tensor_tensor(out=hr[s][:, 0:n], in0=tmp[:, 0:n], scalar=gate, in1=hr[s][:, 0:n],
                                                              op0=ALU.mult, op1=ALU.add),
                      reads=['res_tmp', ('hr', s), 'modsb'], writes=[('hr', s)])
            cx.dma('sp', self.H[f0:f0 + 128, c0:c0 + n], hr[s][:, 0:n], reads=[('hr', s)], writes=[('H', ti, blk)])
        if bias_fn is not None:
            self._res_tmp = self.sb(st, "r_tmp", [128, 512], F32)
        return epi

    def copy_h_blocks(self, hsrc, blks):
        for blk in blks:
            self.cx.dma('sp', self.H[:, blk * 512:(blk + 1) * 512], hsrc[:, blk * 512:(blk + 1) * 512], writes=[('Hc', blk)])

    def mlp(self, i, blks):
        cx = self.cx
        W1, k1 = self.need('mlp_w1', i)
        W2, k2 = self.need('mlp_w2', i)
        W1v = W1.rearrange("(kc p) f -> p kc f", p=128)
        W2v = W2.rearrange("(kc p) f -> p kc f", p=128)
        uv = self.U.rearrange("(kc p) t -> p kc t", p=128)
        with ExitStack() as st:
            xin = [self.sb(st, "m_xin%d" % k, [128, 16, 512], BF16) for k in range(2)]
            hT = self.sb(st, "m_hT", [128, 64, 512], BF16)
            wp = [self.sb(st, "m_wp%d" % k, [128, 8192], BF16) for k in range(3)]
            rl = [self.sb(st, "m_rl%d" % k, [128, 512], F32) for k in range(2)]
            res_epi = self.make_res_epi(st, 1, self.H, nslots=2)
            nw = 0
            self._mw = 0
            nr = 0
            hv = self.H.rearrange("(kc p) t -> p kc t", p=128)
            xt = self.sb(st, "m_xt", [128, 16, 512], F32)
            sqc = [self.sb(st, "m_sq%d" % k, [128, 512], BF16) for k in range(2)]
            rstd = self.sb(st, "m_rstd", [128, 512], F32)

            def norm_load(bi_):
                b_ = blks[bi_]
                cx.dma('sp', xt[:, :, :], hv[:, :, b_ * 512:(b_ + 1) * 512], writes=['mxt'])

            SB = 7

            def n_sq(bi_, j):
                for kc in (2 * j, 2 * j + 1):
                    q = kc % 2
                    cx.op('act', lambda e: e.activation(out=sqc[q][:, :], in_=xt[:, kc, :], func=AF.Square), reads=['mxt'], writes=[('msq', q)])

            def n_mm(bi_, j):
                for kc in (2 * j, 2 * j + 1):
                    q = kc % 2
                    cx.op('pe', lambda e: e.matmul(self.ps[SB][:, :], lhsT=self.ones[:, :], rhs=sqc[q][:, :], start=(kc == 0), stop=(kc == 15)),
                          reads=[('msq', q), 'ones'], writes=[('ps', SB)], signal=True)

            def n_rstd(bi_):
                cx.op('act', lambda e: e.activation(out=rstd[:, :], in_=self.ps[SB][:, :], func=AF.Sqrt, scale=1.0 / D, bias=self.epsb(1e-6)[:, :]),
                      reads=[('ps', SB), 'epsb'], writes=['mrstd'])
                cx.op('dve', lambda e: e.reciprocal(out=rstd[:, :], in_=rstd[:, :]), reads=['mrstd'], writes=['mrstd'])

            def n_tail(bi_, q4):
                b_ = blks[bi_]
                s_ = bi_ % 2
                col = blk_mod(b_)
                for kc in range(4 * q4, 4 * q4 + 4):
                    cx.op('dve', lambda e: e.tensor_tensor(out=xt[:, kc, :], in0=xt[:, kc, :], in1=rstd[:, :], op=ALU.mult),
                          reads=['mrstd', 'mxt'], writes=['mxt'])
                    cx.op('act', lambda e: e.activation(out=xin[s_][:, kc, :], in_=xt[:, kc, :], func=AF.Identity,
                                                        scale=self.A[:, 1, kc, col:col + 1], bias=self.m_shift(1, kc, col)),
                          reads=['mxt', 'A', 'modsb'], writes=[('mx', s_)])

            def norm_sched(bi_, pc):
                if 3 <= pc <= 10:
                    n_mm(bi_, pc - 3)
                if 2 <= pc <= 9:
                    n_sq(bi_, pc - 2)
                if pc == 11:
                    n_rstd(bi_)
                if 12 <= pc <= 15:
                    n_tail(bi_, pc - 12)

            def norm_compute(bi_):
                for pc in range(16):
                    norm_sched(bi_, pc)
            norm_load(0)
            norm_compute(0)
            if len(blks) > 1:
                norm_load(1)
            for bi, blk in enumerate(blks):
                xs = bi % 2
                for pc in range(16):
                    ws = self._mw % 3
                    self._mw += 1
                    wv = wp[ws][:, :].rearrange("p (kc w) -> p kc w", kc=16)
                    cx.dma('sp', wv, W1v[:, :, pc * 512:(pc + 1) * 512], reads=k1, writes=[('mw', ws)])
                    self.bg_step()
                    if bi + 1 < len(blks):
                        norm_sched(bi + 1, pc)
                    for fi in range(4):
                        b = self.next_ps(0, 7)
                        pst = self.ps[b][:, :]
                        for kc in range(16):
                            cx.op('pe', lambda e: e.matmul(pst, lhsT=wv[:, kc, fi * 128:(fi + 1) * 128], rhs=xin[xs][:, kc, :],
                                                           start=(kc == 0), stop=(kc == 15)),
                                  reads=[('mw', ws), ('mx', xs)], writes=[('ps', b)], signal=(kc == 15))
                        r = nr % 2
                        nr += 1
                        hc = pc * 4 + fi
                        cx.op('act', lambda e: e.activation(out=rl[r][:, :], in_=pst, func=AF.Relu),
                              reads=[('ps', b)], writes=[('rl', r)])
                        cx.op('dve', lambda e: e.tensor_tensor(out=hT[:, hc, :], in0=rl[r][:, :], in1=rl[r][:, :], op=ALU.mult),
                              reads=[('rl', r)], writes=[('hT', hc)])
                if bi + 2 < len(blks):
                    norm_load(bi + 2)
                ditems = [(fq, kq) for fq in range(4) for kq in range(4)]
                dviews = {}
                dn = [0]

                def dens(k):
                    while dn[0] <= k and dn[0] < len(ditems):
                        fq_, kq_ = ditems[dn[0]]
                        ws_ = nw % 3 if False else (self._mw % 3)
                        self._mw += 1
                        wv_ = wp[ws_][:, :].rearrange("p (kc w) -> p kc w", kc=16)
                        cx.dma('sp', wv_, W2v[:, kq_ * 16:(kq_ + 1) * 16, fq_ * 512:(fq_ + 1) * 512], reads=k2, writes=[('mw', ws_)])
                        dviews[(fq_, kq_)] = (wv_, ws_)
                        dn[0] += 1
                for t, (fq, kq) in enumerate(ditems):
                    dens(t + 2)
                    banks = [(fq % 2) * 4 + k for k in range(4)]
                    wv, ws = dviews.pop((fq, kq))
                    for fi in range(4):
                        b = banks[fi]
                        pst = self.ps[b][:, :]
                        for kc in range(16):
                            last = (kq == 3 and kc == 15)
                            cx.op('pe', lambda e: e.matmul(pst, lhsT=wv[:, kc, fi * 128:(fi + 1) * 128], rhs=hT[:, kq * 16 + kc, :],
                                                           start=(kq == 0 and kc == 0), stop=last),
                                  reads=[('mw', ws), ('hT', kq * 16 + kc)], writes=[('ps', b)], signal=(last or (fi == 3 and kc == 15)))
                    if kq == 3:
                        for fi in range(4):
                            ft = fq * 4 + fi
                            res_epi(blk, ft, ft * 128, 128, self.ps[banks[fi]][:, :], ('ps', banks[fi]), 512)
            cx.barrier()

    def rope(self, bufs, src32, skey, R, pos0, n, cos, sin, rotT, out_ap, okey):
        cx = self.cx
        b = self.next_ps(6, 8)
        pst = self.ps[b][0:R, 0:n]
        cx.op('pe', lambda e: e.matmul(pst, lhsT=rotT[0:R, 0:R], rhs=src32, start=True, stop=True),
              reads=[skey, 'ropec'], writes=[('ps', b)])
        t1 = bufs['t1'][0:R, 0:n]
        t2 = bufs['t2'][0:R, 0:n]
        cx.op('dve', lambda e: e.tensor_tensor(out=t1, in0=src32, in1=cos[0:R, pos0:pos0 + n], op=ALU.mult),
              reads=[skey, 'ropec'], writes=['rt1'])
        cx.op('dve', lambda e: e.tensor_tensor(out=t2, in0=pst, in1=sin[0:R, pos0:pos0 + n], op=ALU.mult),
              reads=[('ps', b), 'ropec'], writes=['rt2'])
        cx.op('pool', lambda e: e.tensor_tensor(out=out_ap, in0=t1, in1=t2, op=ALU.add),
              reads=['rt1', 'rt2'], writes=[okey])

    def rope_bufs(self, st, R, cos_d, sin_d, rot_d):
        cx = self.cx
        bufs = {'t1': self.sb(st, "rp_t1", [128, 512], F32), 't2': self.sb(st, "rp_t2", [128, 512], F32)}
        cos = self.sb(st, "rp_cos", [R, L], F32)
        sin = self.sb(st, "rp_sin", [R, L], F32)
        rot = self.sb(st, "rp_rot", [R, R], F32)
        cx.dma('sp', cos[:, :], cos_d[:, :], writes=['ropec'])
        cx.dma('sp', sin[:, :], sin_d[:, :], writes=['ropec'])
        cx.dma('sp', rot[:, :], rot_d[:, :], writes=['ropec'])
        return bufs, cos, sin, rot

    def make_copy_epi(self, st, dst, row_fn, tag):
        cx = self.cx
        stg = [self.sb(st, "ce_%s%d" % (tag, k), [128, 512], BF16) for k in range(3)]
        cnt = [0]

        def epi(blk, ti, f0, fsz, pst, pskey, n):
            s = cnt[0] % 3
            cnt[0] += 1
            eng = 'act' if cnt[0] % 2 else 'dve'
            if eng == 'act':
                cx.op('act', lambda e: e.activation(out=stg[s][0:fsz, 0:n], in_=pst, func=AF.Copy), reads=[pskey], writes=[(tag, s)])
            else:
                cx.op('dve', lambda e: e.tensor_copy(out=stg[s][0:fsz, 0:n], in_=pst), reads=[pskey], writes=[(tag, s)])
            r0 = row_fn(ti)
            cx.dma('sp', dst[r0:r0 + fsz, blk * 512: blk * 512 + n], stg[s][0:fsz, 0:n], reads=[(tag, s)], writes=[(tag + 'o', ti, blk)])
        return epi

    def make_tm_epi(self, st, dst, tag):
        cx = self.cx
        stg = [self.sb(st, "te_%s%d" % (tag, k), [128, 512], BF16) for k in range(3)]
        cnt = [0]

        def epi(blk, ti, f0, fsz, pst, pskey, n, tt):
            s = cnt[0] % 3
            cnt[0] += 1
            eng = 'act' if cnt[0] % 2 else 'dve'
            if eng == 'act':
                cx.op('act', lambda e: e.activation(out=stg[s][:, 0:fsz], in_=pst, func=AF.Copy), reads=[pskey], writes=[(tag, s)])
            else:
                cx.op('dve', lambda e: e.tensor_copy(out=stg[s][:, 0:fsz], in_=pst), reads=[pskey], writes=[(tag, s)])
            r0 = blk * 512 + tt * 128
            cx.dma('sp', dst[r0:r0 + 128, f0:f0 + fsz], stg[s][:, 0:fsz], reads=[(tag, s)], writes=[(tag + 'o', ti, blk, tt)])
        return epi

    def make_norm_epi(self, st, nch, dim, eps, gvec, g0, dst, tag, extra=None):
        cx = self.cx
        ck = [self.sb(st, "ne_ck%d" % k, [128, nch, 512], F32) for k in range(2)]
        cb = [self.sb(st, "ne_cb%d" % k, [128, nch, 512], BF16) for k in range(2)]
        bufs = {'sq': self.sb(st, "ne_sq", [128, nch * 512], BF16), 'rstd': self.sb(st, "ne_rstd", [128, 512], F32)}
        dv = dst.rearrange("(kc p) t -> p kc t", p=128)

        def epi(blk, ti, f0, fsz, pst, pskey, n):
            if ti >= nch:
                return extra(blk, ti, f0, fsz, pst, pskey, n)
            s = blk % 2
            cx.op('act', lambda e: e.activation(out=ck[s][:, ti, 0:n], in_=pst, func=AF.Copy), reads=[pskey], writes=[('ck', s)])
            if ti == nch - 1:
                rstd, rk = self.fm_rstd(bufs, ck[s][:, :, 0:n], ('ck', s), nch, n, dim, eps)
                for kc in range(nch):
                    cx.op('dve', lambda e: e.scalar_tensor_tensor(out=cb[s][:, kc, 0:n], in0=ck[s][:, kc, 0:n],
                                                                  scalar=gvec[:, g0 + kc:g0 + kc + 1], in1=rstd,
                                                                  op0=ALU.mult, op1=ALU.mult),
                          reads=[('ck', s), rk, 'gvec'], writes=[('cb', s)])
                cx.dma('sp', dv[:, :, blk * 512: blk * 512 + n], cb[s][:, :, 0:n], reads=[('cb', s)], writes=[(tag, blk)])
        return epi

    def mla(self, i, j, hsrc, need_ctx):
        cx = self.cx
        Kc = self.K
        CKVN = self.dscr("CKVN%d" % i, [512, T], BF16)
        KR = self.dscr("KR%d" % i, [64, T], BF16)
        KN = self.dscr("KN%d" % i, [D, T], BF16)
        V = self.dscr("V%d" % i, [T, D], BF16)
        CQN = self.dscr("CQN%d" % i, [768, T], BF16)
        QN = self.dscr("QN%d" % i, [D, T], BF16)
        QR = self.dscr("QR%d" % i, [1024, T], BF16)
        O = self.dscr("O%d" % i, [D, T], BF16)
        uv = self.U.rearrange("(kc p) t -> p kc t", p=128)
        allb = [(b, 512) for b in range(NBLK)]
        qb = allb if need_ctx else allb[:8]
        with ExitStack() as st:
            kvng = self.sb(st, "kvng", [128, 8], F32)
            cx.dma('sp', kvng[:, :], self.v_kvng[:, :], writes=['gvec'])
            rb, cos, sin, rot = self.rope_bufs(st, 64, Kc['k_cosa'], Kc['k_sina'], Kc['k_rota'])
            kr32 = [self.sb(st, "kr32_%d" % k, [64, 512], F32) for k in range(2)]
            krb = [self.sb(st, "krb_%d" % k, [64, 512], BF16) for k in range(2)]
            cnt = [0]

            def extra(blk, ti, f0, fsz, pst, pskey, n):
                s = cnt[0] % 2
                cnt[0] += 1
                if blk < 8:
                    cx.op('act', lambda e: e.activation(out=kr32[s][:, 0:n], in_=pst, func=AF.Copy), reads=[pskey], writes=[('kr32', s)])
                    self.rope(rb, kr32[s][:, 0:n], ('kr32', s), 64, (blk % 4) * 512, n, cos, sin, rot, krb[s][:, 0:n], ('krb', s))
                else:
                    cx.op('act', lambda e: e.activation(out=krb[s][:, 0:n], in_=pst, func=AF.Copy), reads=[pskey], writes=[('krb', s)])
                cx.dma('sp', KR[:, blk * 512: blk * 512 + n], krb[s][:, 0:n], reads=[('krb', s)], writes=[('KR', blk)])
            epi = self.make_norm_epi(st, 4, 512, 1e-6, kvng, j * 4, CKVN, 'CKVN', extra)
            Wd, wk = self.need('mla_w_dkv', j)
            ft = [(f * 128, 128) for f in range(4)] + [(512, 64)]
            self.gemm(st, Wd, wk, 16, ft, allb, lambda blk: uv[:, :, blk * 512:(blk + 1) * 512], epi)
            cx.barrier()
        Wd, wk = self.need('mla_w_ukv', j)
        cv = CKVN.rearrange("(kc p) t -> p kc t", p=128)
        with ExitStack() as st:
            epi = self.make_copy_epi(st, KN, lambda ti: ti * 128, 'kn')
            self.gemm(st, Wd[:, 0:2048], wk, 4, [(f * 128, 128) for f in range(16)], allb,
                      lambda blk: cv[:, :, blk * 512:(blk + 1) * 512], epi)
            cx.barrier()
        with ExitStack() as st:
            epi = self.make_tm_epi(st, V, 'v')
            self.gemm(st, Wd[:, 2048:4096], wk, 4, [(f * 512, 512) for f in range(4)], allb,
                      lambda blk: cv[:, :, blk * 512:(blk + 1) * 512], epi, token_major=True)
            cx.barrier()
        with ExitStack() as st:
            qng = self.sb(st, "qng", [128, 12], F32)
            cx.dma('sp', qng[:, :], self.v_qng[:, :], writes=['gvec'])
            epi = self.make_norm_epi(st, 6, 768, 1e-6, qng, j * 6, CQN, 'CQN')
            Wd, wk = self.need('mla_w_dq', j)
            self.gemm(st, Wd, wk, 16, [(f * 128, 128) for f in range(6)], qb,
                      lambda blk: uv[:, :, blk * 512:(blk + 1) * 512], epi)
            cx.barrier()
        with ExitStack() as st:
            rb, cos, sin, rot = self.rope_bufs(st, 64, Kc['k_cosa'], Kc['k_sina'], Kc['k_rota'])
            q32 = [self.sb(st, "q32_%d" % k, [64, 512], F32) for k in range(2)]
            qrb = [self.sb(st, "qrb_%d" % k, [64, 512], BF16) for k in range(2)]
            nope_epi = self.make_copy_epi(st, QN, lambda ti: (ti // 2) * 128, 'qn')
            cnt = [0]

            def epi(blk, ti, f0, fsz, pst, pskey, n):
                if ti % 2 == 0:
                    return nope_epi(blk, ti, f0, fsz, pst, pskey, n)
                h = ti // 2
                s = cnt[0] % 2
                cnt[0] += 1
                if blk < 8:
                    cx.op('act', lambda e: e.activation(out=q32[s][:, 0:n], in_=pst, func=AF.Copy), reads=[pskey], writes=[('q32', s)])
                    self.rope(rb, q32[s][:, 0:n], ('q32', s), 64, (blk % 4) * 512, n, cos, sin, rot, qrb[s][:, 0:n], ('qrb', s))
                else:
                    cx.op('act', lambda e: e.activation(out=qrb[s][:, 0:n], in_=pst, func=AF.Copy), reads=[pskey], writes=[('qrb', s)])
                cx.dma('sp', QR[h * 64:(h + 1) * 64, blk * 512: blk * 512 + n], qrb[s][:, 0:n], reads=[('qrb', s)], writes=[('QR', h, blk)])
            ft = []
            for h in range(16):
                ft += [(h * 192, 128), (h * 192 + 128, 64)]
            Wd, wk = self.need('mla_w_uq', j)
            qv = CQN.rearrange("(kc p) t -> p kc t", p=128)
            self.gemm(st, Wd, wk, 6, ft, qb, lambda blk: qv[:, :, blk * 512:(blk + 1) * 512], epi)
            cx.barrier()
        self.attention(KN, KR, V, QN, QR, O, need_ctx, 192 ** -0.5)
        with ExitStack() as st:
            epi = self.make_res_epi(st, 0, hsrc)
            Wd, wk = self.need('mla_w_o', j)
            ov = O.rearrange("(kc p) t -> p kc t", p=128)
            self.gemm(st, Wd, wk, 16, [(f * 128, 128) for f in range(16)], qb,
                      lambda blk: ov[:, :, blk * 512:(blk + 1) * 512], epi)
            cx.barrier()

    def attention(self, KN, KR, V, QN, QR, O, need_ctx, scale):
        cx = self.cx
        with ExitStack() as st:
            Vsb = self.sb(st, "a_V", [128, 18, D], BF16)
            krs = self.sb(st, "a_kr", [128, 2304], BF16)
            kn = [self.sb(st, "a_kn%d" % k, [128, 2304], BF16) for k in range(2)]
            qn = [self.sb(st, "a_qn%d" % k, [128, 2304], BF16) for k in range(2)]
            qr = [self.sb(st, "a_qr%d" % k, [128, 2304], BF16) for k in range(2)]
            et = [self.sb(st, "a_et%d" % k, [128, 512], BF16) for k in range(4)]
            rd = [self.sb(st, "a_rd%d" % k, [128, 512], F32) for k in range(2)]
            ob = [self.sb(st, "a_ob%d" % k, [128, 512], BF16) for k in range(2)]
            cx.op('pool', lambda e: e.memset(krs[64:128, :], 0.0), writes=['akrz'])
            for k in range(2):
                cx.op('pool', lambda e: e.memset(qr[k][64:128, :], 0.0), writes=[('aqrz', k)])
            ne = 0
            nq = 0
            nsb = 0

            def load_head(b, h):
                lat0 = b * L
                ctx0 = NBATCH * L + b * C
                s = h % 2
                for (dstt, src, rows, r0, key) in ((kn[s], KN, 128, h * 128, ('akn', s)), (qn[s], QN, 128, h * 128, ('aqn', s)),
                                                  (qr[s], QR, 64, h * 64, ('aqr', s))):
                    cx.dma('sp', dstt[0:rows, 0:L], src[r0:r0 + rows, lat0:lat0 + L], writes=[key])
                    cx.dma('sp', dstt[0:rows, L:L + C], src[r0:r0 + rows, ctx0:ctx0 + C], writes=[key])
            heads = [(b, h) for b in range(NBATCH) for h in range(16)]
            load_head(0, 0)
            for hi, (b, h) in enumerate(heads):
                lat0 = b * L
                ctx0 = NBATCH * L + b * C
                if h == 0:
                    cx.dma('sp', Vsb[:, 0:16, :], V[lat0:lat0 + L, :].rearrange("(c p) f -> p c f", p=128), writes=['aV'])
                    cx.dma('sp', Vsb[:, 16:18, :], V[ctx0:ctx0 + C, :].rearrange("(c p) f -> p c f", p=128), writes=['aV'])
                    cx.dma('sp', krs[0:64, 0:L], KR[:, lat0:lat0 + L], writes=['akr'])
                    cx.dma('sp', krs[0:64, L:L + C], KR[:, ctx0:ctx0 + C], writes=['akr'])
                if hi + 1 < len(heads):
                    load_head(*heads[hi + 1])
                s = h % 2
                qblocks = [(q * 512, 512, list(range(18))) for q in range(4)]
                if need_ctx:
                    qblocks.append((L, C, [16, 17]))
                for (q0, nqq, kcs) in qblocks:
                    self.bg_step(3)
                    bo = 4 + (nq % 2)
                    bd = 6 + (nq % 2)
                    r = nq % 2
                    nq += 1
                    pso = self.ps[bo][:, 0:nqq]
                    psd = self.ps[bd][:, 0:nqq]
                    prev = None

                    def pv(prev, first, last):
                        kc, slot = prev
                        cx.op('pe', lambda e: e.matmul(pso, lhsT=Vsb[:, kc, h * 128:(h + 1) * 128], rhs=et[slot][:, 0:nqq],
                                                       start=first, stop=last),
                              reads=['aV', ('et', slot)], writes=[('ps', bo)], signal=last)
                        cx.op('pe', lambda e: e.matmul(psd, lhsT=self.ones[:, :], rhs=et[slot][:, 0:nqq], start=first, stop=last),
                              reads=['ones', ('et', slot)], writes=[('ps', bd)], signal=last)
                    for idx, kc in enumerate(kcs):
                        bs = nsb % 4
                        nsb += 1
                        pss = self.ps[bs][:, 0:nqq]
                        cx.op('pe', lambda e: e.matmul(pss, lhsT=kn[s][:, kc * 128:(kc + 1) * 128], rhs=qn[s][:, q0:q0 + nqq],
                                                       start=True, stop=False),
                              reads=[('akn', s), ('aqn', s)], writes=[('ps', bs)], signal=False)
                        cx.op('pe', lambda e: e.matmul(pss, lhsT=krs[:, kc * 128:(kc + 1) * 128], rhs=qr[s][:, q0:q0 + nqq],
                                                       start=False, stop=True),
                              reads=['akr', 'akrz', ('aqr', s), ('aqrz', s)], writes=[('ps', bs)], signal=True)
                        if prev is not None:
                            pv(prev, idx == 1, False)
                        slot = ne % 4
                        ne += 1
                        cx.op('act', lambda e: e.activation(out=et[slot][:, 0:nqq], in_=pss, func=AF.Exp, scale=float(scale)),
                              reads=[('ps', bs)], writes=[('et', slot)])
                        prev = (kc, slot)
                    pv(prev, len(kcs) == 1, True)
                    cx.op('dve', lambda e: e.reciprocal(out=rd[r][:, 0:nqq], in_=psd), reads=[('ps', bd)], writes=[('rd', r)])
                    cx.op('dve', lambda e: e.tensor_tensor(out=ob[r][:, 0:nqq], in0=pso, in1=rd[r][:, 0:nqq], op=ALU.mult),
                          reads=[('ps', bo), ('rd', r)], writes=[('ob', r)])
                    c0 = (lat0 + q0) if q0 < L else ctx0
                    cx.dma('sp', O[h * 128:(h + 1) * 128, c0:c0 + nqq], ob[r][:, 0:nqq], reads=[('ob', r)], writes=[('O', h, c0)])
            cx.barrier()

    def diffattn(self, i, j, hsrc, need_ctx):
        cx = self.cx
        Kc = self.K
        lambda_init = 0.8 - 0.6 * math.exp(-0.3 * i)
        QD = self.dscr("QD%d" % i, [D, T], BF16)
        KD = self.dscr("KD%d" % i, [D, T], BF16)
        V = self.dscr("VD%d" % i, [T, D], BF16)
        O = self.dscr("OD%d" % i, [D, T], BF16)
        uv = self.U.rearrange("(kc p) t -> p kc t", p=128)
        allb = [(b, 512) for b in range(NBLK)]
        qb = allb if need_ctx else allb[:8]
        Wd, wk = self.need('df_w_qkv', j)
        with ExitStack() as st:
            rb, cos, sin, rot = self.rope_bufs(st, 128, Kc['k_cosd'], Kc['k_sind'], Kc['k_rotd'])
            x32 = [self.sb(st, "d32_%d" % k, [128, 512], F32) for k in range(2)]
            xb = [self.sb(st, "dxb_%d" % k, [128, 512], BF16) for k in range(2)]
            cnt = [0]

            def epi(blk, ti, f0, fsz, pst, pskey, n):
                s = cnt[0] % 2
                cnt[0] += 1
                dst = QD if ti < 16 else KD
                r0 = (ti % 16) * 128
                if blk < 8:
                    cx.op('act', lambda e: e.activation(out=x32[s][:, 0:n], in_=pst, func=AF.Copy), reads=[pskey], writes=[('x32', s)])
                    self.rope(rb, x32[s][:, 0:n], ('x32', s), 128, (blk % 4) * 512, n, cos, sin, rot, xb[s][:, 0:n], ('xb', s))
                else:
                    cx.op('act', lambda e: e.activation(out=xb[s][:, 0:n], in_=pst, func=AF.Copy), reads=[pskey], writes=[('xb', s)])
                cx.dma('sp', dst[r0:r0 + 128, blk * 512: blk * 512 + n], xb[s][:, 0:n], reads=[('xb', s)], writes=[('QK', ti, blk)])
            self.gemm(st, Wd, wk, 16, [(f * 128, 128) for f in range(32)], allb,
                      lambda blk: uv[:, :, blk * 512:(blk + 1) * 512], epi)
            cx.barrier()
        with ExitStack() as st:
            epi = self.make_tm_epi(st, V, 'v')
            self.gemm(st, Wd[:, 4096:6144], wk, 16, [(f * 512, 512) for f in range(4)], allb,
                      lambda blk: uv[:, :, blk * 512:(blk + 1) * 512], epi, token_major=True)
            cx.barrier()
        scale = 128 ** -0.5
        with ExitStack() as st:
            Vsb = self.sb(st, "f_V", [128, 18, D], BF16)
            kd = [[self.sb(st, "f_kd%d%d" % (k, c), [128, 2304], BF16) for c in range(2)] for k in range(2)]
            qd = [[self.sb(st, "f_qd%d%d" % (k, c), [128, 2304], BF16) for c in range(2)] for k in range(2)]
            et = [self.sb(st, "f_et%d" % k, [128, 512], BF16) for k in range(4)]
            rd = [self.sb(st, "f_rd%d" % k, [128, 512], F32) for k in range(2)]
            od = self.sb(st, "f_od", [128, 2, 512], F32)
            tmp = self.sb(st, "f_tmp", [128, 2, 512], F32)
            obf = [self.sb(st, "f_ob%d" % k, [128, 2, 512], BF16) for k in range(2)]
            bufs = {'sq': self.sb(st, "f_sq", [128, 1024], BF16), 'rstd': self.sb(st, "f_rstd", [128, 512], F32)}
            lt = self.sb(st, "f_lt", [1, 512], F32)
            lw = self.sb(st, "f_lw", [1, 256], F32)
            lsm = self.sb(st, "f_ls", [1, 4], F32)
            lamn = self.sb(st, "f_lamn", [128, 2], F32)
            sg = self.sb(st, "f_sg", [128, 2], F32)
            cx.dma('sp', lt[:, :], self.v_lam[:, :], writes=['lt'])
            cx.dma('sp', sg[:, :], self.v_subg[:, :], writes=['sg'])
            cx.op('dve', lambda e: e.tensor_tensor(out=lw[:, 0:128], in0=lt[:, 0:128], in1=lt[:, 128:256], op=ALU.mult), reads=['lt'], writes=['lw'])
            cx.op('dve', lambda e: e.tensor_tensor(out=lw[:, 128:256], in0=lt[:, 256:384], in1=lt[:, 384:512], op=ALU.mult), reads=['lt'], writes=['lw'])
            cx.op('dve', lambda e: e.tensor_reduce(out=lsm[:, 0:1], in_=lw[:, 0:128], axis=mybir.AxisListType.X, op=ALU.add), reads=['lw'], writes=['lsm'])
            cx.op('dve', lambda e: e.tensor_reduce(out=lsm[:, 1:2], in_=lw[:, 128:256], axis=mybir.AxisListType.X, op=ALU.add), reads=['lw'], writes=['lsm'])
            cx.op('act', lambda e: e.activation(out=lsm[:, 0:2], in_=lsm[:, 0:2], func=AF.Exp), reads=['lsm'], writes=['lsm'])
            cx.op('dve', lambda e: e.scalar_tensor_tensor(out=lsm[:, 2:3], in0=lsm[:, 1:2], scalar=-float(lambda_init), in1=lsm[:, 0:1],
                                                          op0=ALU.add, op1=ALU.subtract), reads=['lsm'], writes=['lsm'])
            cx.op('dve', lambda e: e.tensor_copy(out=lsm[:, 3:4], in_=lsm[:, 2:3]), reads=['lsm'], writes=['lsm'])
            bb = self.next_ps(6, 8)
            cx.op('pe', lambda e: e.matmul(self.ps[bb][:, 0:2], lhsT=self.onesf[0:1, :], rhs=lsm[:, 2:4], start=True, stop=True),
                  reads=['lsm', 'onesf'], writes=[('ps', bb)])
            cx.op('act', lambda e: e.activation(out=lamn[:, :], in_=self.ps[bb][:, 0:2], func=AF.Copy), reads=[('ps', bb)], writes=['lamn'])
            cx.op('dve', lambda e: e.tensor_scalar(out=sg[:, :], in0=sg[:, :], scalar1=float(1.0 - lambda_init), scalar2=None, op0=ALU.mult),
                  reads=['sg'], writes=['gvec'])
            ne = 0
            nq = 0
            nsb = [0]
            tail = [None]
            for b in range(NBATCH):
                lat0 = b * L
                ctx0 = NBATCH * L + b * C
                if tail[0] is not None:
                    tail[0]()
                    tail[0] = None
                cx.dma('sp', Vsb[:, 0:16, :], V[lat0:lat0 + L, :].rearrange("(c p) f -> p c f", p=128), writes=['aV'])
                cx.dma('sp', Vsb[:, 16:18, :], V[ctx0:ctx0 + C, :].rearrange("(c p) f -> p c f", p=128), writes=['aV'])
                def load_head(b2, h2):
                    l0 = b2 * L
                    x0_ = NBATCH * L + b2 * C
                    s2 = h2 % 2
                    for c in range(2):
                        r0 = (h2 * 2 + c) * 128
                        for (dstt, src, key) in ((kd[s2][c], KD, ('akd', s2, c)), (qd[s2][c], QD, ('aqd', s2, c))):
                            cx.dma('sp', dstt[:, 0:L], src[r0:r0 + 128, l0:l0 + L], writes=[key])
                            cx.dma('sp', dstt[:, L:L + C], src[r0:r0 + 128, x0_:x0_ + C], writes=[key])
                if b == 0:
                    load_head(0, 0)
                for h in range(8):
                    s = h % 2
                    if h + 1 < 8:
                        load_head(b, h + 1)
                    elif b + 1 < NBATCH:
                        load_head(b + 1, 0)
                    qblocks = [(q * 512, 512, list(range(18))) for q in range(4)]
                    if need_ctx:
                        qblocks.append((L, C, [16, 17]))
                    for (q0, nqq, kcs) in qblocks:
                        self.bg_step(5)
                        r = nq % 2
                        nq += 1
                        c0 = (lat0 + q0) if q0 < L else ctx0
                        for c in range(2):
                            bo = [2 + 2 * c, 3 + 2 * c]
                            bd = 6 + c
                            psd = self.ps[bd][:, 0:nqq]
                            prev = None

                            def pv(prev, first, last):
                                kc, slot = prev
                                for half in range(2):
                                    cx.op('pe', lambda e: e.matmul(self.ps[bo[half]][:, 0:nqq],
                                                                   lhsT=Vsb[:, kc, h * 256 + half * 128: h * 256 + (half + 1) * 128],
                                                                   rhs=et[slot][:, 0:nqq], start=first, stop=last),
                                          reads=['aV', ('et', slot)], writes=[('ps', bo[half])], signal=last)
                                cx.op('pe', lambda e: e.matmul(psd, lhsT=self.ones[:, :], rhs=et[slot][:, 0:nqq], start=first, stop=last),
                                      reads=['ones', ('et', slot)], writes=[('ps', bd)], signal=last)
                            for idx, kc in enumerate(kcs):
                                bs = nsb[0] % 2
                                nsb[0] += 1
                                pss = self.ps[bs][:, 0:nqq]
                                cx.op('pe', lambda e: e.matmul(pss, lhsT=kd[s][c][:, kc * 128:(kc + 1) * 128], rhs=qd[s][c][:, q0:q0 + nqq],
                                                               start=True, stop=True),
                                      reads=[('akd', s, c), ('aqd', s, c)], writes=[('ps', bs)], signal=True)
                                if prev is not None:
                                    pv(prev, idx == 1, False)
                                slot = ne % 4
                                ne += 1
                                cx.op('act', lambda e: e.activation(out=et[slot][:, 0:nqq], in_=pss, func=AF.Exp, scale=float(scale)),
                                      reads=[('ps', bs)], writes=[('et', slot)])
                                prev = (kc, slot)
                                if c == 0 and idx == min(4, len(kcs) - 1) and tail[0] is not None:
                                    tail[0]()
                                    tail[0] = None
                            pv(prev, len(kcs) == 1, True)
                            if c == 0:
                                cx.op('dve', lambda e: e.reciprocal(out=rd[0][:, 0:nqq], in_=self.ps[6][:, 0:nqq]), reads=[('ps', 6)], writes=[('rd', 0)])
                                for half in range(2):
                                    cx.op('dve', lambda e: e.tensor_tensor(out=od[:, half, 0:nqq], in0=self.ps[2 + half][:, 0:nqq], in1=rd[0][:, 0:nqq], op=ALU.mult),
                                          reads=[('ps', 2 + half), ('rd', 0)], writes=['od'])

                        def make_tail(nqq=nqq, r=r, h=h, c0=c0):
                            def t():
                                cx.op('dve', lambda e: e.reciprocal(out=rd[1][:, 0:nqq], in_=self.ps[7][:, 0:nqq]), reads=[('ps', 7)], writes=[('rd', 1)])
                                cx.op('dve', lambda e: e.tensor_scalar(out=rd[1][:, 0:nqq], in0=rd[1][:, 0:nqq], scalar1=lamn[:, 0:1], scalar2=None, op0=ALU.mult),
                                      reads=[('rd', 1), 'lamn'], writes=[('rd', 1)])
                                for half in range(2):
                                    cx.op('dve', lambda e: e.tensor_tensor(out=tmp[:, half, 0:nqq], in0=self.ps[4 + half][:, 0:nqq], in1=rd[1][:, 0:nqq], op=ALU.mult),
                                          reads=[('ps', 4 + half), ('rd', 1)], writes=['tmp'])
                                    cx.op('pool', lambda e: e.tensor_tensor(out=od[:, half, 0:nqq], in0=od[:, half, 0:nqq], in1=tmp[:, half, 0:nqq], op=ALU.add),
                                          reads=['od', 'tmp'], writes=['od'])
                                bs2 = nsb[0] % 2
                                nsb[0] += 1
                                rstd, rk = self.fm_rstd(bufs, od[:, :, 0:nqq], 'od', 2, nqq, 256, 1e-5, bank=bs2)
                                for half in range(2):
                                    cx.op('dve', lambda e: e.scalar_tensor_tensor(out=obf[r][:, half, 0:nqq], in0=od[:, half, 0:nqq],
                                                                                  scalar=sg[:, half:half + 1], in1=rstd, op0=ALU.mult, op1=ALU.mult),
                                          reads=['od', rk, 'gvec'], writes=[('obf', r)])
                                cx.dma('sp', O[h * 256:(h + 1) * 256, c0:c0 + nqq].rearrange("(two p) t -> p two t", p=128), obf[r][:, :, 0:nqq],
                                       reads=[('obf', r)], writes=[('O', h, c0)])
                            return t
                        tail[0] = make_tail()
            if tail[0] is not None:
                tail[0]()
                tail[0] = None
            cx.barrier()
        with ExitStack() as st:
            epi = self.make_res_epi(st, 0, hsrc)
            Wd, wk = self.need('df_w_o', j)
            ov = O.rearrange("(kc p) t -> p kc t", p=128)
            self.gemm(st, Wd, wk, 16, [(f * 128, 128) for f in range(16)], qb,
                      lambda blk: ov[:, :, blk * 512:(blk + 1) * 512], epi)
            cx.barrier()

    def hyena(self, i, j, hsrc, need_ctx):
        cx = self.cx
        Kc = self.K
        ZP = self.dscr("ZP", [3 * D, T], BF16)
        G = self.dscr("HG", [D, T], BF16)
        X0 = self.dscr("HX0", [D, T], BF16)
        GT = self.dscr("HGT", [T, D], BF16)
        YO = self.dscr("HYO", [D, T], BF16)
        uv = self.U.rearrange("(kc p) t -> p kc t", p=128)
        allb = [(b, 512) for b in range(NBLK)]
        qb = allb if need_ctx else allb[:8]
        tags = [('l', L)] + ([('c', C)] if need_ctx else [])
        seqs = [(0, L, 'l'), (L, L, 'l')] + ([(2 * L, C, 'c'), (2 * L + C, C, 'c')] if need_ctx else [])
        P = {}
        for tag, Lx in tags:
            n2 = 2 * Lx
            FA = self.dscr("HFA" + tag, [Lx, D], BF16)
            FB = self.dscr("HFB" + tag, [Lx, D], BF16)
            P1 = self.dscr("HP1" + tag, [Lx, D], F32)
            P2 = self.dscr("HP2" + tag, [Lx, D], F32)
            P3 = self.dscr("HP3" + tag, [Lx, D], F32)
            P[tag] = (P1, P2, P3)
            with ExitStack() as st:
                w1 = self.sb(st, "hf_w1", [33, 64], F32)
                w2 = self.sb(st, "hf_w2", [64, 64], F32)
                w3 = self.sb(st, "hf_w3", [64, 64], F32)
                fv = self.sb(st, "hf_fv", [64, 6], F32)
                wo = self.sb(st, "hf_wo", [64, 2 * D], F32)
                ft_ = self.sb(st, "hf_ft", [33, Lx], F32)
                hh = [self.sb(st, "hf_h%d" % k, [64, Lx], F32) for k in range(3)]
                a32 = self.sb(st, "hf_a", [64, 512], F32)
                m1 = self.sb(st, "hf_m1", [64, 512], F32)
                m2 = self.sb(st, "hf_m2", [64, 512], F32)
                dec = [self.sb(st, "hf_dec%d" % k, [128, D], F32) for k in range(2)]
                hf = self.sb(st, "hf_hf", [128, D], F32)
                hb = self.sb(st, "hf_hb", [128, D], F32)
                Ab = [self.sb(st, "hf_A%d" % k, [128, D], BF16) for k in range(2)]
                Bb = [self.sb(st, "hf_B%d" % k, [128, D], BF16) for k in range(2)]
                for (t_, d_) in ((w1, self.f_w1), (w2, self.f_w2), (w3, self.f_w3), (fv, self.f_vec), (wo, self.f_wout), (ft_, Kc['k_feat' + tag])):
                    cx.dma('sp', t_[:, :], d_[:, :], writes=['hfw'])
                srcs = [(ft_, 33, w1), (hh[0], 64, w2), (hh[1], 64, w3)]
                for li, (src, kk, wl) in enumerate(srcs):
                    for c0 in range(0, Lx, 512):
                        n = min(512, Lx - c0)
                        b = self.next_ps(0, 6)
                        pst = self.ps[b][0:64, 0:n]
                        cx.op('pe', lambda e: e.matmul(pst, lhsT=wl[0:kk, 0:64], rhs=src[0:kk, c0:c0 + n], start=True, stop=True),
                              reads=['hfw', ('hh', li - 1)], writes=[('ps', b)])
                        cx.op('dve', lambda e: e.tensor_scalar(out=a32[:, 0:n], in0=pst, scalar1=fv[:, li:li + 1], scalar2=fv[:, 3 + li:4 + li],
                                                               op0=ALU.add, op1=ALU.mult), reads=[('ps', b), 'hfw'], writes=['a32'])
                        cx.op('dve', lambda e: e.tensor_scalar(out=m1[:, 0:n], in0=a32[:, 0:n], scalar1=PI, scalar2=-2.0 * PI,
                                                               op0=ALU.is_gt, op1=ALU.mult), reads=['a32'], writes=['m1'])
                        cx.op('dve', lambda e: e.tensor_scalar(out=m2[:, 0:n], in0=a32[:, 0:n], scalar1=-PI, scalar2=2.0 * PI,
                                                               op0=ALU.is_lt, op1=ALU.mult), reads=['a32'], writes=['m2'])
                        cx.op('dve', lambda e: e.tensor_tensor(out=m1[:, 0:n], in0=m1[:, 0:n], in1=m2[:, 0:n], op=ALU.add), reads=['m1', 'm2'], writes=['m1'])
                        cx.op('dve', lambda e: e.tensor_tensor(out=a32[:, 0:n], in0=a32[:, 0:n], in1=m1[:, 0:n], op=ALU.add), reads=['a32', 'm1'], writes=['a32'])
                        cx.op('act', lambda e: e.activation(out=hh[li][:, c0:c0 + n], in_=a32[:, 0:n], func=AF.Sin), reads=['a32'], writes=[('hh', li)])
                for tt in range(Lx // 128):
                    s = tt % 2
                    cx.dma('sp', dec[s][:, :], Kc['k_dec' + tag][tt * 128:(tt + 1) * 128, :], writes=[('dec', s)])
                    for gq in range(8):
                        b = self.next_ps(0, 6)
                        pst = self.ps[b][:, :]
                        cx.op('pe', lambda e: e.matmul(pst, lhsT=hh[2][:, tt * 128:(tt + 1) * 128], rhs=wo[:, gq * 512:(gq + 1) * 512], start=True, stop=True),
                              reads=[('hh', 2), 'hfw'], writes=[('ps', b)])
                        dst = hf if gq < 4 else hb
                        dc = (gq % 4) * 512
                        cx.op('dve', lambda e: e.tensor_tensor(out=dst[:, dc:dc + 512], in0=pst, in1=dec[s][:, dc:dc + 512], op=ALU.mult),
                              reads=[('ps', b), ('dec', s)], writes=['hf' if gq < 4 else 'hb'])
                    if tt == 0:
                        cx.op('dve', lambda e: e.memset(hb[0:1, :], 0.0), writes=['hb'])
                    cx.op('pool', lambda e: e.tensor_tensor(out=Ab[s][:, :], in0=hf[:, :], in1=hb[:, :], op=ALU.add), reads=['hf', 'hb'], writes=[('Ab', s)])
                    cx.op('pool', lambda e: e.tensor_tensor(out=Bb[s][:, :], in0=hb[:, :], in1=hf[:, :], op=ALU.subtract), reads=['hf', 'hb'], writes=[('Bb', s)])
                    cx.dma('sp', FA[tt * 128:(tt + 1) * 128, :], Ab[s][:, :], reads=[('Ab', s)], writes=[('FA', tt)])
                    cx.dma('sp', FB[tt * 128:(tt + 1) * 128, :], Bb[s][:, :], reads=[('Bb', s)], writes=[('FB', tt)])
                cx.barrier()
            KCx = Lx // 128
            nft = Lx // 128
            dblocks = [(q, 512) for q in range(4)]
            fav = FA.rearrange("(kc p) d -> p kc d", p=128)
            fbv = FB.rearrange("(kc p) d -> p kc d", p=128)
            for which in range(3):
                with ExitStack() as st:
                    stg = [self.sb(st, "hs_stg%d" % k, [128, 512], F32) for k in range(3)]
                    cnt = [0]

                    def epi(blk, ti, f0, fsz, pst, pskey, n):
                        s = cnt[0] % 3
                        cnt[0] += 1
                        dc = blk * 512
                        if which == 2:
                            cx.op('act', lambda e: e.activation(out=stg[s][0:1, 0:n], in_=pst[0:1, :], func=AF.Copy, scale=1.0 / n2),
                                  reads=[pskey], writes=[('stg', s)])
                            cx.dma('sp', P3[0:1, dc:dc + n], stg[s][0:1, 0:n], reads=[('stg', s)], writes=[('P3n', blk)])
                            return
                        cx.op('act', lambda e: e.activation(out=stg[s][:, 0:n], in_=pst, func=AF.Copy, scale=2.0 / n2),
                              reads=[pskey], writes=[('stg', s)])
                        r0 = ti * 128
                        if which == 0:
                            if ti == 0:
                                cx.op('dve', lambda e: e.tensor_scalar(out=stg[s][0:1, 0:n], in0=stg[s][0:1, 0:n], scalar1=0.5, scalar2=None, op0=ALU.mult),
                                      reads=[('stg', s)], writes=[('stg', s)])
                                cx.dma('sp', P3[1:128, dc:dc + n], stg[s][1:128, 0:n], reads=[('stg', s)], writes=[('P3', ti, blk)])
                            else:
                                cx.dma('sp', P3[r0:r0 + 128, dc:dc + n], stg[s][:, 0:n], reads=[('stg', s)], writes=[('P3', ti, blk)])
                            cx.dma('sp', P1[r0:r0 + 128, dc:dc + n], stg[s][:, 0:n], reads=[('stg', s)], writes=[('P1', ti, blk)])
                        else:
                            if ti == 0:
                                cx.op('dve', lambda e: e.memset(stg[s][0:1, 0:n], 0.0), reads=[('stg', s)], writes=[('stg', s)])
                            cx.dma('sp', P2[r0:r0 + 128, dc:dc + n], stg[s][:, 0:n], reads=[('stg', s)], writes=[('P2', ti, blk)])
                    if which == 0:
                        self.gemm(st, Kc['k_wf' + tag], [], KCx, [(f * 256, 128) for f in range(nft)], dblocks,
                                  lambda blk: fav[:, :, blk * 512:(blk + 1) * 512], epi)
                    elif which == 1:
                        self.gemm(st, Kc['k_wf' + tag], [], KCx, [(f * 256 + 128, 128) for f in range(nft)], dblocks,
                                  lambda blk: fbv[:, :, blk * 512:(blk + 1) * 512], epi)
                    else:
                        self.gemm(st, Kc['k_sg' + tag], [], KCx, [(0, 2)], dblocks,
                                  lambda blk: fav[:, :, blk * 512:(blk + 1) * 512], epi)
                    cx.barrier()
        with ExitStack() as st:
            hv = self.sb(st, "hy_bin", [128, 48], F32)
            cx.dma('sp', hv[:, :], self.v_hybin[:, :], writes=['hyvec'])
            stg = [self.sb(st, "hz_stg%d" % k, [128, 512], BF16) for k in range(3)]
            cnt = [0]

            def epi(blk, ti, f0, fsz, pst, pskey, n):
                s = cnt[0] % 3
                cnt[0] += 1
                cx.op('act', lambda e: e.activation(out=stg[s][:, 0:n], in_=pst, func=AF.Identity, bias=hv[:, ti:ti + 1]),
                      reads=[pskey, 'hyvec'], writes=[('stg', s)])
                cx.dma('sp', ZP[f0:f0 + 128, blk * 512: blk * 512 + n], stg[s][:, 0:n], reads=[('stg', s)], writes=[('ZP', ti, blk)])
            Wd, wk = self.need('hy_w_in', j)
            self.gemm(st, Wd, wk, 16, [(f * 128, 128) for f in range(48)], qb,
                      lambda blk: uv[:, :, blk * 512:(blk + 1) * 512], epi)
            cx.barrier()
        with ExitStack() as st:
            cw = self.sb(st, "hc_cw", [128, 144], F32)
            cb = self.sb(st, "hc_cb", [128, 48], F32)
            cx.dma('sp', cw[:, :], self.v_hycw[:, :], writes=['hyvec'])
            cx.dma('sp', cb[:, :], self.v_hycb[:, :], writes=['hyvec'])
            yp = [[self.sb(st, "hc_yp%d%d" % (k, p_), [128, L + 2], BF16) for p_ in range(3)] for k in range(2)]
            z2 = [[self.sb(st, "hc_z%d%d" % (k, p_), [128, L], F32) for p_ in range(3)] for k in range(2)]
            g322 = [self.sb(st, "hc_g32%d" % k, [128, L], F32) for k in range(2)]
            gb2 = [self.sb(st, "hc_gb%d" % k, [128, L], BF16) for k in range(2)]
            x0b2 = [self.sb(st, "hc_x0b%d" % k, [128, L], BF16) for k in range(2)]
            gtb = [self.sb(st, "hc_gtb%d" % k, [128, 512], BF16) for k in range(2)]
            for k in range(2):
                for p_ in range(3):
                    cx.op('pool', lambda e: e.memset(yp[k][p_][:, 0:1], 0.0), writes=[('yp', k, p_)])
            ns = 0
            ng = 0
            units = [(dt, sq) for dt in range(16) for sq in seqs]

            def load_unit(u):
                dt_, (c0_, Ls_, tag_) = units[u]
                k_ = u % 2
                for p_ in range(3):
                    ti_ = p_ * 16 + dt_
                    cx.dma('sp', yp[k_][p_][:, 1:Ls_ + 1], ZP[ti_ * 128:(ti_ + 1) * 128, c0_:c0_ + Ls_], writes=[('yp', k_, p_)])
                    cx.op('pool', lambda e: e.memset(yp[k_][p_][:, Ls_ + 1:Ls_ + 2], 0.0), writes=[('yp', k_, p_)])
            load_unit(0)
            for ui, (dt, (c0, Ls, tag)) in enumerate(units):
                if True:
                    k = ns % 2
                    ns += 1
                    z = z2[k]
                    g32 = g322[k]
                    gb = gb2[k]
                    x0b = x0b2[k]
                    if ui + 1 < len(units):
                        load_unit(ui + 1)
                    self.bg_step(1)
                    for p_ in range(3):
                        ti = p_ * 16 + dt
                        cx.op('act', lambda e: e.activation(out=z[p_][:, 0:Ls], in_=yp[k][p_][:, 1:Ls + 1], func=AF.Identity,
                                                            scale=cw[:, 48 + ti:48 + ti + 1], bias=cb[:, ti:ti + 1]),
                              reads=[('yp', k, p_), 'hyvec'], writes=[('z', k, p_)])
                        cx.op('dve', lambda e: e.scalar_tensor_tensor(out=z[p_][:, 0:Ls], in0=yp[k][p_][:, 0:Ls], scalar=cw[:, ti:ti + 1],
                                                                      in1=z[p_][:, 0:Ls], op0=ALU.mult, op1=ALU.add),
                              reads=[('yp', k, p_), ('z', k, p_), 'hyvec'], writes=[('z', k, p_)])
                        cx.op('dve', lambda e: e.scalar_tensor_tensor(out=z[p_][:, 0:Ls], in0=yp[k][p_][:, 2:Ls + 2], scalar=cw[:, 96 + ti:96 + ti + 1],
                                                                      in1=z[p_][:, 0:Ls], op0=ALU.mult, op1=ALU.add),
                              reads=[('yp', k, p_), ('z', k, p_), 'hyvec'], writes=[('z', k, p_)])
                    cx.op('pool', lambda e: e.tensor_tensor(out=g32[:, 0:Ls], in0=z[2][:, 0:Ls], in1=z[1][:, 0:Ls], op=ALU.mult),
                          reads=[('z', k, 1), ('z', k, 2)], writes=[('g32', k)])
                    cx.op('act', lambda e: e.activation(out=gb[:, 0:Ls], in_=g32[:, 0:Ls], func=AF.Copy), reads=[('g32', k)], writes=[('gb', k)])
                    cx.op('act', lambda e: e.activation(out=x0b[:, 0:Ls], in_=z[0][:, 0:Ls], func=AF.Copy), reads=[('z', k, 0)], writes=[('x0b', k)])
                    cx.dma('sp', G[dt * 128:(dt + 1) * 128, c0:c0 + Ls], gb[:, 0:Ls], reads=[('gb', k)], writes=[('G', dt, c0)])
                    cx.dma('sp', X0[dt * 128:(dt + 1) * 128, c0:c0 + Ls], x0b[:, 0:Ls], reads=[('x0b', k)], writes=[('X0', dt, c0)])
                    for q0 in range(0, Ls, 512):
                        nn = min(512, Ls - q0)
                        nk = nn // 128
                        b = self.next_ps(0, 8)
                        for kk in range(nk):
                            cx.op('pe', lambda e: e.transpose(self.ps[b][:, kk * 128:(kk + 1) * 128], g32[:, q0 + kk * 128: q0 + (kk + 1) * 128], self.ident[:, :]),
                                  reads=[('g32', k), 'ident'], writes=[('ps', b)], signal=(kk == nk - 1))
                        r = ng % 2
                        ng += 1
                        cx.op('dve', lambda e: e.tensor_copy(out=gtb[r][:, 0:nn], in_=self.ps[b][:, 0:nn]), reads=[('ps', b)], writes=[('gtb', r)])
                        cx.dma('sp', GT[c0 + q0:c0 + q0 + nn, dt * 128:(dt + 1) * 128].rearrange("(k p) j -> p k j", p=128),
                               gtb[r][:, 0:nn].rearrange("p (k j) -> p k j", k=nk), reads=[('gtb', r)], writes=[('GT', dt, c0, q0)])
            cx.barrier()
        YS = {}
        for (c0, Ls, tag) in seqs:
            P1, P2, P3 = P[tag]
            ys_d = self.dscr("HYS%d" % c0, [2 * Ls, D], BF16)
            YS[c0] = ys_d
            KCx = Ls // 128
            nft = Ls // 128
            with ExitStack() as st:
                gc = [self.sb(st, "hd_gc%d" % k, [128, 512], F32) for k in range(4)]
                pt = [[self.sb(st, "hd_p%d%d" % (k, q), [128, 512], F32) for q in range(3)] for k in range(4)]
                ta = self.sb(st, "hd_ta", [128, 512], F32)
                tb_ = self.sb(st, "hd_tb", [128, 512], F32)
                tc_ = self.sb(st, "hd_tc", [128, 512], F32)
                td = self.sb(st, "hd_td", [128, 512], F32)
                ycb = [self.sb(st, "hd_yc%d" % k, [128, 512], BF16) for k in range(4)]
                ysb = [self.sb(st, "hd_ys%d" % k, [128, 512], BF16) for k in range(4)]
                cnt = [0]

                def epi(blk, ti, f0, fsz, pst, pskey, n):
                    ft = ti // 2
                    s = blk % 4
                    dc = blk * 512
                    if ti % 2 == 0:
                        cx.op('act', lambda e: e.activation(out=gc[s][:, 0:n], in_=pst, func=AF.Copy), reads=[pskey], writes=[('gc', s)])
                        for q, Pq in enumerate((P1, P2, P3)):
                            if q == 2 and ft != 0:
                                continue
                            cx.dma('sp', pt[s][q][:, 0:n], Pq[ft * 128:(ft + 1) * 128, dc:dc + n], writes=[('pt', s, q)])
                        return
                    p3 = pt[s][2] if ft == 0 else pt[s][0]
                    p3k = ('pt', s, 2) if ft == 0 else ('pt', s, 0)
                    cnt[0] += 1
                    cx.op('pool', lambda e: e.tensor_tensor(out=ta[:, 0:n], in0=gc[s][:, 0:n], in1=pt[s][0][:, 0:n], op=ALU.mult),
                          reads=[('gc', s), ('pt', s, 0)], writes=['ta'])
                    cx.op('dve', lambda e: e.tensor_tensor(out=tb_[:, 0:n], in0=pst, in1=pt[s][1][:, 0:n], op=ALU.mult),
                          reads=[pskey, ('pt', s, 1)], writes=['tb'])
                    cx.op('dve', lambda e: e.tensor_tensor(out=tc_[:, 0:n], in0=pst, in1=p3[:, 0:n], op=ALU.mult),
                          reads=[pskey, p3k], writes=['tc'])
                    cx.op('pool', lambda e: e.tensor_tensor(out=td[:, 0:n], in0=gc[s][:, 0:n], in1=pt[s][1][:, 0:n], op=ALU.mult),
                          reads=[('gc', s), ('pt', s, 1)], writes=['td'])
                    cx.op('dve', lambda e: e.tensor_tensor(out=ycb[s][:, 0:n], in0=ta[:, 0:n], in1=tb_[:, 0:n], op=ALU.add),
                          reads=['ta', 'tb'], writes=[('ycb', s)])
                    cx.op('pool', lambda e: e.tensor_tensor(out=ysb[s][:, 0:n], in0=tc_[:, 0:n], in1=td[:, 0:n], op=ALU.subtract),
                          reads=['tc', 'td'], writes=[('ysb', s)])
                    cx.dma('sp', ys_d[ft * 128:(ft + 1) * 128, dc:dc + n], ycb[s][:, 0:n], reads=[('ycb', s)], writes=[('YSc', ft, blk)])
                    cx.dma('sp', ys_d[Ls + ft * 128:Ls + (ft + 1) * 128, dc:dc + n], ysb[s][:, 0:n], reads=[('ysb', s)], writes=[('YSs', ft, blk)])
                ftl = []
                for f in range(nft):
                    ftl += [(f * 256, 128), (f * 256 + 128, 128)]
                gtv = GT[c0:c0 + Ls, :].rearrange("(kc p) d -> p kc d", p=128)
                self.gemm(st, Kc['k_wf' + tag], [], KCx, ftl, [(q, 512) for q in range(4)],
                          lambda blk: gtv[:, :, blk * 512:(blk + 1) * 512], epi)
                cx.barrier()
        for (c0, Ls, tag) in seqs:
            ys_d = YS[c0]
            KCx = 2 * Ls // 128
            with ExitStack() as st:
                fb = self.sb(st, "hi_fb", [128, 16], F32)
                cx.dma('sp', fb[:, :], self.v_hyfb[:, :], writes=['hyvec'])
                gt_ = [self.sb(st, "hi_g%d" % k, [128, 512], BF16) for k in range(2)]
                xt_ = [self.sb(st, "hi_x%d" % k, [128, 512], BF16) for k in range(2)]
                t32 = [self.sb(st, "hi_t%d" % k, [128, 512], F32) for k in range(2)]
                yob = [self.sb(st, "hi_yo%d" % k, [128, 512], BF16) for k in range(2)]
                cnt = [0]

                def epi(blk, ti, f0, fsz, pst, pskey, n):
                    s = cnt[0] % 2
                    cnt[0] += 1
                    cc = c0 + blk * 512
                    cx.dma('sp', gt_[s][:, 0:n], G[f0:f0 + 128, cc:cc + n], writes=[('gt', s)])
                    cx.dma('sp', xt_[s][:, 0:n], X0[f0:f0 + 128, cc:cc + n], writes=[('xt', s)])
                    cx.op('dve', lambda e: e.scalar_tensor_tensor(out=t32[s][:, 0:n], in0=gt_[s][:, 0:n], scalar=fb[:, ti:ti + 1], in1=pst,
                                                                  op0=ALU.mult, op1=ALU.add),
                          reads=[pskey, ('gt', s), 'hyvec'], writes=[('t32', s)])
                    cx.op('pool', lambda e: e.tensor_tensor(out=yob[s][:, 0:n], in0=t32[s][:, 0:n], in1=xt_[s][:, 0:n], op=ALU.mult),
                          reads=[('t32', s), ('xt', s)], writes=[('yob', s)])
                    cx.dma('sp', YO[f0:f0 + 128, cc:cc + n], yob[s][:, 0:n], reads=[('yob', s)], writes=[('YO', ti, cc)])
                csv = Kc['k_csi' + tag].rearrange("(kc p) t -> p kc t", p=128)
                tbl = [(q, min(512, Ls - q * 512)) for q in range((Ls + 511) // 512)]
                self.gemm(st, ys_d, [], KCx, [(f * 128, 128) for f in range(16)], tbl,
                          lambda blk: csv[:, :, blk * 512: blk * 512 + min(512, Ls - blk * 512)], epi, xel=32768)
                cx.barrier()
        with ExitStack() as st:
            bo = self.sb(st, "hy_bout", [128, 16], F32)
            cx.dma('sp', bo[:, :], self.v_hybout[:, :], writes=['hyvec'])
            epi = self.make_res_epi(st, 0, hsrc, bias_fn=lambda ti: bo[:, ti:ti + 1])
            Wd, wk = self.need('hy_w_out', j)
            yv = YO.rearrange("(kc p) t -> p kc t", p=128)
            self.gemm(st, Wd, wk, 16, [(f * 128, 128) for f in range(16)], qb,
                      lambda blk: yv[:, :, blk * 512:(blk + 1) * 512], epi)
            cx.barrier()


def prep_shared(inp):
    f = np.float32
    m = {}
    for k in ('ada_w', 'mlp_w1', 'mlp_w2', 'mla_w_dq', 'mla_w_uq', 'mla_w_dkv', 'mla_w_ukv', 'mla_w_o',
              'hy_w_in', 'hy_w_out', 'df_w_qkv', 'df_w_o'):
        m[k] = np.ascontiguousarray(inp[k], dtype=f)
    m['v_adab'] = np.concatenate([_pm(inp['ada_b'][i]) for i in range(DEPTH)], 1).astype(f)
    m['v_normg'] = np.concatenate([_pm(inp['norm_g'][i, w]) for i in range(DEPTH) for w in range(2)], 1).astype(f)
    m['v_fing'] = _pm(inp['final_norm_g']).astype(f)
    m['v_qng'] = np.concatenate([_pm(inp['mla_q_norm_g'][j]) for j in range(2)], 1).astype(f)
    m['v_kvng'] = np.concatenate([_pm(inp['mla_kv_norm_g'][j]) for j in range(2)], 1).astype(f)
    m['v_hybin'] = _pm(inp['hy_b_in'][0]).astype(f)
    m['v_hycw'] = np.concatenate([_pm(inp['hy_conv_w'][0, k]) for k in range(3)], 1).astype(f)
    m['v_hycb'] = _pm(inp['hy_conv_b'][0]).astype(f)
    m['v_hyfb'] = _pm(inp['hy_filt_bias'][0]).astype(f)
    m['v_hybout'] = _pm(inp['hy_b_out'][0]).astype(f)
    m['v_subg'] = _pm(inp['df_subln_g'][0]).astype(f)
    m['v_lam'] = np.ascontiguousarray(inp['df_lambda'][0].reshape(1, 512), dtype=f)
    m['f_w1'] = np.ascontiguousarray(inp['hy_filt_w1'][0], dtype=f)
    m['f_w2'] = np.ascontiguousarray(inp['hy_filt_w2'][0], dtype=f)
    m['f_w3'] = np.ascontiguousarray(inp['hy_filt_w3'][0], dtype=f)
    m['f_vec'] = np.ascontiguousarray(np.stack([inp['hy_filt_b1'][0], inp['hy_filt_b2'][0], inp['hy_filt_b3'][0],
                                                inp['hy_filt_freq'][0, 0], inp['hy_filt_freq'][0, 1],
                                                inp['hy_filt_freq'][0, 2]], 1), dtype=f)
    m['f_wout'] = np.ascontiguousarray(inp['hy_filt_wout'][0], dtype=f)
    m.update(_constants())
    return m


def prep_core(inp, core):
    b0 = core * NBATCH
    x, ctx, c = inp['x'], inp['ctx'], inp['c']
    xT = np.concatenate([x[b0].T, x[b0 + 1].T, ctx[b0].T, ctx[b0 + 1].T], axis=1)
    cc = np.stack([c[b0], c[b0 + 1], inp['c_ctx']], -1)
    cT = np.ascontiguousarray(cc.reshape(16, 128, 3).transpose(1, 0, 2))
    return {'xT': np.ascontiguousarray(xT, dtype=np.float32), 'cT': cT.astype(np.float32)}


_PROG = {}


def kernel(**inputs):
    inp = {k: np.asarray(v) for k, v in inputs.items()}
    if 'full' not in _PROG:
        _PROG['full'] = Builder().build()
    nc = _PROG['full']
    shared = prep_shared(inp)
    in_maps = []
    for core in range(NCORES):
        m = dict(shared)
        m.update(prep_core(inp, core))
        in_maps.append(m)
    res = run_bass_kernel_spmd(nc, in_maps, core_ids=list(range(NCORES)))
    out = np.empty((NCORES * NBATCH, L, D), np.float32)
    for core in range(NCORES):
        o = res.results[core]["OUT"]
        for b in range(NBATCH):
            out[core * NBATCH + b] = o[:, b * L:(b + 1) * L].T
    return out
```
